# Optimizing a Trainium2 kernel written in Bass

```python
import math
import jax, jax.numpy as jnp
from jax import lax
import numpy as np

D_MODEL = 1024
BATCH = 16
SEQ = 256
DEPTH = 4
DEC_BATCH = 4
DEC_SEQ = 2048
PAST_LEN = 512

GRID_W = 64
HEAD_DIM = 64
GROUP_W = D_MODEL // 4
N_HEADS_GRP = GROUP_W // HEAD_DIM
DA_SUB = HEAD_DIM // 2
CONV_K = 3
MLA_Q_RANK = 256
MLA_KV_RANK = 128
MLA_NOPE = 64
MLA_ROPE = 32
MLA_V = 64
MLA_SCALE = (MLA_NOPE + MLA_ROPE) ** -0.5
NA_WIN_H = 8
NA_WIN_W = 16
NA_COL_BLOCK = 16
NA_BAND_W = 2 * NA_WIN_W
ROPE_BASE = 10000.0
Q_BLOCK = 128
EPS = 1e-6
NEG_INF = -1e30
IN_SIZES = (GROUP_W,) * 8 + (MLA_Q_RANK, MLA_KV_RANK, MLA_ROPE, GROUP_W) + (GROUP_W,) * 4
D_IN = sum(IN_SIZES)

kernel_name = 'hybrid_parallel_groups_flow_step'


def rms_norm(x, g):
    xf = x.astype(jnp.float32)
    y = xf * lax.rsqrt(jnp.mean(xf * xf, axis=-1, keepdims=True) + EPS)
    return (y * g.astype(jnp.float32)).astype(x.dtype)


def adaln(cvec, w, b):
    m = jax.nn.silu(cvec) @ w + b
    return jnp.split(m, 3, axis=-1)


def in_projection(x, shift, scale, norm_g, w_in):
    h = rms_norm(x, norm_g) * (1.0 + scale) + shift
    u = h @ w_in
    parts, off = [], 0
    for n in IN_SIZES:
        parts.append(u[..., off:off + n])
        off += n
    return parts


def out_projection(ys, zs, w_out):
    y = jnp.concatenate([yb * jax.nn.silu(z) for yb, z in zip(ys, zs)], axis=-1)
    return y @ w_out


def axial_rope_tables(T, rot_dim):
    t = jnp.arange(T)
    rows = (t // GRID_W).astype(jnp.float32)
    cols = (t % GRID_W).astype(jnp.float32)
    half = rot_dim // 2
    inv = 1.0 / (ROPE_BASE ** (jnp.arange(0, half, 2, dtype=jnp.float32) / half))
    ar = rows[:, None] * inv
    ac = cols[:, None] * inv
    ang = jnp.concatenate([ar, ar, ac, ac], axis=-1)
    return jnp.cos(ang), jnp.sin(ang)


def apply_axial_rope(x, cos, sin):
    T, R = cos.shape
    q4 = R // 4
    xs = x.reshape(*x.shape[:-1], 2, 2, q4)
    rot = jnp.stack([-xs[..., 1, :], xs[..., 0, :]], axis=-2).reshape(x.shape)
    bshape = (1, T) + (1,) * (x.ndim - 3) + (R,)
    out = x.astype(jnp.float32) * cos.reshape(bshape) + rot.astype(jnp.float32) * sin.reshape(bshape)
    return out.astype(x.dtype)


def map_query_blocks(fn, *qs):
    bsz, S = qs[0].shape[:2]
    nb = S // Q_BLOCK
    blocks = tuple(jnp.swapaxes(q.reshape(bsz, nb, Q_BLOCK, *q.shape[2:]), 0, 1) for q in qs)
    out = lax.map(lambda blk: fn(*blk), blocks)
    out = jnp.swapaxes(out, 0, 1)
    return out.reshape(bsz, S, *out.shape[3:])


def dense_attend(q, k, v, scale):
    s = jnp.einsum('bqhd,bkhd->bhqk', q, k).astype(jnp.float32) * scale
    p = jax.nn.softmax(s, axis=-1)
    return jnp.einsum('bhqk,bkhe->bqhe', p.astype(v.dtype), v)


def diff_lambda(lam_vecs, lam_init):
    lv = lam_vecs.astype(jnp.float32)
    return jnp.exp(jnp.sum(lv[0] * lv[1])) - jnp.exp(jnp.sum(lv[2] * lv[3])) + lam_init


def diff_attend(q, k, v, lam, lam_init, subln_g):
    s = jnp.einsum('bqhcd,bkhcd->bchqk', q, k).astype(jnp.float32) * (DA_SUB ** -0.5)
    p = jax.nn.softmax(s, axis=-1)
    attn = p[:, 0] - lam * p[:, 1]
    o = jnp.einsum('bhqk,bkhe->bqhe', attn.astype(v.dtype), v)
    return rms_norm(o, subln_g) * (1.0 - lam_init)


def short_conv(u, w):
    return lax.conv_general_dilated(
        u, w[:, None, :], window_strides=(1,), padding=[(CONV_K // 2, CONV_K // 2)],
        dimension_numbers=('NWC', 'WIO', 'NWC'), feature_group_count=u.shape[-1])


def mla_compress(cq, ckv, p):
    bsz, T = cq.shape[:2]
    q = (rms_norm(cq, p['mla_q_norm_g']) @ p['mla_w_uq']).reshape(bsz, T, N_HEADS_GRP, MLA_NOPE + MLA_ROPE)
    return q[..., :MLA_NOPE], q[..., MLA_NOPE:], rms_norm(ckv, p['mla_kv_norm_g'])


def mla_expand(ckv_n, w_ukv):
    bsz, K = ckv_n.shape[:2]
    kv = (ckv_n @ w_ukv).reshape(bsz, K, N_HEADS_GRP, MLA_NOPE + MLA_V)
    return kv[..., :MLA_NOPE], kv[..., MLA_NOPE:]


def mla_attend(qn, qp, kn, kp, v):
    s = (jnp.einsum('bqhd,bkhd->bhqk', qn, kn)
         + jnp.einsum('bqhr,bkr->bhqk', qp, kp)).astype(jnp.float32) * MLA_SCALE
    p = jax.nn.softmax(s, axis=-1)
    return jnp.einsum('bhqk,bkhe->bqhe', p.astype(v.dtype), v)


def neighbourhood_attend(q, k, v, ctx_k, ctx_v, rpb):
    bsz, T, H, d = q.shape
    rows = T // GRID_W
    wr = min(NA_WIN_H, rows)
    ncb = GRID_W // NA_COL_BLOCK
    r = np.arange(rows)
    row_idx = np.clip(r - wr // 2, 0, rows - wr)[:, None] + np.arange(wr)
    j = np.arange(ncb)
    band_start = np.clip(j * NA_COL_BLOCK - NA_WIN_W // 2, 0, GRID_W - NA_BAND_W)
    col_idx = band_start[:, None] + np.arange(NA_BAND_W)
    qcol = j[:, None] * NA_COL_BLOCK + np.arange(NA_COL_BLOCK)
    win_start = np.clip(qcol - NA_WIN_W // 2, 0, GRID_W - NA_WIN_W)
    col_mask = ((col_idx[:, None, :] >= win_start[:, :, None])
                & (col_idx[:, None, :] < win_start[:, :, None] + NA_WIN_W))
    dy = row_idx - r[:, None] + NA_WIN_H - 1
    dx = np.clip(col_idx[:, None, :] - qcol[:, :, None] + NA_WIN_W - 1, 0, 2 * NA_WIN_W - 2)
    bias = rpb[:, dy[:, None, None, :, None], dx[None, :, :, None, :]].astype(jnp.float32)
    bias = jnp.where(col_mask[None, None, :, :, None, :], bias, NEG_INF)
    bias = bias.transpose(1, 2, 0, 3, 4, 5).reshape(rows, ncb, H, NA_COL_BLOCK, wr * NA_BAND_W)

    qg = q.reshape(bsz, rows, ncb, NA_COL_BLOCK, H, d)
    ri = row_idx[:, None, :, None]
    ci = col_idx[None, :, None, :]

    def gather_band(a):
        g = a.reshape(bsz, rows, GRID_W, H, d)[:, ri, ci]
        return g.transpose(0, 1, 2, 5, 3, 4, 6).reshape(bsz, rows, ncb, H, wr * NA_BAND_W, d)

    kb = gather_band(k)
    vb = gather_band(v)
    scale = d ** -0.5
    s_loc = jnp.einsum('brjqhd,brjhkd->brjhqk', qg, kb).astype(jnp.float32) * scale + bias[None]
    s_ctx = jnp.einsum('brjqhd,blhd->brjhql', qg, ctx_k).astype(jnp.float32) * scale
    p = jax.nn.softmax(jnp.concatenate([s_loc, s_ctx], axis=-1), axis=-1)
    k_loc = wr * NA_BAND_W
    o = (jnp.einsum('brjhqk,brjhkd->brjqhd', p[..., :k_loc].astype(v.dtype), vb)
         + jnp.einsum('brjhql,blhd->brjqhd', p[..., k_loc:].astype(v.dtype), ctx_v))
    return o.reshape(bsz, T, H, d)


def context_layer(x, mod, p, lam_init):
    bsz, L, _ = x.shape
    H = N_HEADS_GRP
    shift, scale, gate = mod
    (aq, ak, av, az, bb, bc, bh, bz, cq, ckv, ckpe, cz, dq, dk, dv, dz) = in_projection(
        x, shift, scale, p['norm_g'], p['w_in'])
    aq = aq.reshape(bsz, L, H, 2, DA_SUB)
    ak = ak.reshape(bsz, L, H, 2, DA_SUB)
    av = av.reshape(bsz, L, H, HEAD_DIM)
    lam = diff_lambda(p['da_lambda'], lam_init)
    ya = map_query_blocks(lambda qb: diff_attend(qb, ak, av, lam, lam_init, p['da_subln_g']), aq)
    yb = bb * short_conv(bc * bh, p['conv_w'])
    qn, qp, ckv_n = mla_compress(cq, ckv, p)
    kn, vc = mla_expand(ckv_n, p['mla_w_ukv'])
    yc = map_query_blocks(lambda a, b: mla_attend(a, b, kn, ckpe, vc), qn, qp)
    dq = dq.reshape(bsz, L, H, HEAD_DIM)
    dk = dk.reshape(bsz, L, H, HEAD_DIM)
    dv = dv.reshape(bsz, L, H, HEAD_DIM)
    yd = map_query_blocks(lambda qb: dense_attend(qb, dk, dv, HEAD_DIM ** -0.5), dq)
    y = out_projection([ya.reshape(bsz, L, GROUP_W), yb, yc.reshape(bsz, L, GROUP_W), yd.reshape(bsz, L, GROUP_W)],
                       [az, bz, cz, dz], p['w_out'])
    x = x + gate * y
    return x, (ak.reshape(bsz, L, H, HEAD_DIM), av, ckv_n, ckpe, dk, dv)


def latent_layer(x, mod, p, lam_init, ctx):
    bsz, T, _ = x.shape
    H = N_HEADS_GRP
    ctx_ak, ctx_av, ctx_ckv, ctx_kpe, ctx_dk, ctx_dv = ctx
    L = ctx_ak.shape[1]
    shift, scale, gate = mod
    (aq, ak, av, az, bb, bc, bh, bz, cq, ckv, ckpe, cz, dq, dk, dv, dz) = in_projection(
        x, shift, scale, p['norm_g'], p['w_in'])
    cos_a, sin_a = axial_rope_tables(T, DA_SUB)
    aq = apply_axial_rope(aq.reshape(bsz, T, H, 2, DA_SUB), cos_a, sin_a)
    ak = apply_axial_rope(ak.reshape(bsz, T, H, 2, DA_SUB), cos_a, sin_a)
    k_all = jnp.concatenate([ctx_ak.reshape(bsz, L, H, 2, DA_SUB), ak], axis=1)
    v_all = jnp.concatenate([ctx_av, av.reshape(bsz, T, H, HEAD_DIM)], axis=1)
    lam = diff_lambda(p['da_lambda'], lam_init)
    ya = map_query_blocks(lambda qb: diff_attend(qb, k_all, v_all, lam, lam_init, p['da_subln_g']), aq)
    yb = bb * short_conv(bc * bh, p['conv_w'])
    cos_c, sin_c = axial_rope_tables(T, MLA_ROPE)
    qn, qp, ckv_n = mla_compress(cq, ckv, p)
    qp = apply_axial_rope(qp, cos_c, sin_c)
    kp_all = jnp.concatenate([ctx_kpe, apply_axial_rope(ckpe, cos_c, sin_c)], axis=1)
    kn, vc = mla_expand(jnp.concatenate([ctx_ckv, ckv_n], axis=1), p['mla_w_ukv'])
    yc = map_query_blocks(lambda a, b: mla_attend(a, b, kn, kp_all, vc), qn, qp)
    yd = neighbourhood_attend(dq.reshape(bsz, T, H, HEAD_DIM), dk.reshape(bsz, T, H, HEAD_DIM),
                              dv.reshape(bsz, T, H, HEAD_DIM), ctx_dk, ctx_dv, p['na_rpb'])
    y = out_projection([ya.reshape(bsz, T, GROUP_W), yb, yc.reshape(bsz, T, GROUP_W), yd.reshape(bsz, T, GROUP_W)],
                       [az, bz, cz, dz], p['w_out'])
    return x + gate * y


def setup_inputs(seed: int = 0) -> dict:
    key = jax.random.key(seed)
    ks = jax.random.split(key, 24)
    H = N_HEADS_GRP

    def nrm(k, shape, s):
        return jax.random.normal(k, shape, jnp.float32) * s

    return {
        'x_prompt': nrm(ks[0], (BATCH, SEQ, D_MODEL), 1.0),
        'x_sample': nrm(ks[1], (DEC_BATCH, DEC_SEQ, D_MODEL), 1.0),
        'cache_a_k': nrm(ks[2], (DEC_BATCH, DEPTH, PAST_LEN, H, HEAD_DIM), 1.0),
        'cache_a_v': nrm(ks[3], (DEC_BATCH, DEPTH, PAST_LEN, H, HEAD_DIM), 1.0),
        'cache_c_kv': nrm(ks[4], (DEC_BATCH, DEPTH, PAST_LEN, MLA_KV_RANK), 1.0),
        'cache_c_kpe': nrm(ks[5], (DEC_BATCH, DEPTH, PAST_LEN, MLA_ROPE), 1.0),
        'cache_d_k': nrm(ks[6], (DEC_BATCH, DEPTH, PAST_LEN, H, HEAD_DIM), 1.0),
        'cache_d_v': nrm(ks[7], (DEC_BATCH, DEPTH, PAST_LEN, H, HEAD_DIM), 1.0),
        'c': nrm(ks[8], (DEC_BATCH, D_MODEL), 1.0),
        'c_ctx': nrm(ks[9], (D_MODEL,), 1.0),
        'ada_w': nrm(ks[10], (DEPTH, D_MODEL, 3 * D_MODEL), 0.5 * D_MODEL ** -0.5),
        'ada_b': nrm(ks[11], (DEPTH, 3 * D_MODEL), 0.01),
        'norm_g': 1.0 + nrm(ks[12], (DEPTH, D_MODEL), 0.05),
        'w_in': nrm(ks[13], (DEPTH, D_MODEL, D_IN), D_MODEL ** -0.5),
        'da_lambda': nrm(ks[14], (DEPTH, 4, DA_SUB), 0.1),
        'da_subln_g': 1.0 + nrm(ks[15], (DEPTH, HEAD_DIM), 0.05),
        'conv_w': nrm(ks[16], (DEPTH, CONV_K, GROUP_W), CONV_K ** -0.5),
        'mla_q_norm_g': 1.0 + nrm(ks[17], (DEPTH, MLA_Q_RANK), 0.05),
        'mla_w_uq': nrm(ks[18], (DEPTH, MLA_Q_RANK, H * (MLA_NOPE + MLA_ROPE)), MLA_Q_RANK ** -0.5),
        'mla_kv_norm_g': 1.0 + nrm(ks[19], (DEPTH, MLA_KV_RANK), 0.05),
        'mla_w_ukv': nrm(ks[20], (DEPTH, MLA_KV_RANK, H * (MLA_NOPE + MLA_V)), MLA_KV_RANK ** -0.5),
        'na_rpb': nrm(ks[21], (DEPTH, H, 2 * NA_WIN_H - 1, 2 * NA_WIN_W - 1), 0.1),
        'w_out': nrm(ks[22], (DEPTH, D_MODEL, D_MODEL), D_MODEL ** -0.5),
        'final_norm_g': 1.0 + nrm(ks[23], (D_MODEL,), 0.05),
    }


def reference(x_prompt, x_sample, cache_a_k, cache_a_v, cache_c_kv, cache_c_kpe, cache_d_k, cache_d_v,
              c, c_ctx, ada_w, ada_b, norm_g, w_in, da_lambda, da_subln_g, conv_w,
              mla_q_norm_g, mla_w_uq, mla_kv_norm_g, mla_w_ukv, na_rpb, w_out, final_norm_g):
    xp = x_prompt
    xs = x_sample
    s_ak, s_av, s_ckv, s_kpe, s_dk, s_dv = [], [], [], [], [], []
    for l in range(DEPTH):
        p = {
            'norm_g': norm_g[l], 'w_in': w_in[l], 'da_lambda': da_lambda[l], 'da_subln_g': da_subln_g[l],
            'conv_w': conv_w[l], 'mla_q_norm_g': mla_q_norm_g[l], 'mla_w_uq': mla_w_uq[l],
            'mla_kv_norm_g': mla_kv_norm_g[l], 'mla_w_ukv': mla_w_ukv[l], 'na_rpb': na_rpb[l],
            'w_out': w_out[l],
        }
        lam_init = 0.8 - 0.6 * math.exp(-0.3 * l)
        mod_ctx = adaln(c_ctx, ada_w[l], ada_b[l])
        xp, (ak, av, ckv_n, kpe, dk, dv) = context_layer(xp, mod_ctx, p, lam_init)
        s_ak.append(ak); s_av.append(av); s_ckv.append(ckv_n)
        s_kpe.append(kpe); s_dk.append(dk); s_dv.append(dv)
        shift, scale, gate = adaln(c, ada_w[l], ada_b[l])
        mod_lat = (shift[:, None, :], scale[:, None, :], gate[:, None, :])
        ctx_l = (cache_a_k[:, l], cache_a_v[:, l], cache_c_kv[:, l], cache_c_kpe[:, l],
                 cache_d_k[:, l], cache_d_v[:, l])
        xs = latent_layer(xs, mod_lat, p, lam_init, ctx_l)
    y_prompt = rms_norm(xp, final_norm_g)
    y_sample = rms_norm(xs, final_norm_g)
    state_a_k = jnp.stack(s_ak, axis=1)
    state_a_v = jnp.stack(s_av, axis=1)
    state_c_kv = jnp.stack(s_ckv, axis=1)
    state_c_kpe = jnp.stack(s_kpe, axis=1)
    state_d_k = jnp.stack(s_dk, axis=1)
    state_d_v = jnp.stack(s_dv, axis=1)
    return (y_prompt, y_sample, state_a_k, state_a_v, state_c_kv, state_c_kpe, state_d_k, state_d_v)
```

```python
import math
from contextlib import ExitStack
import numpy as np
import concourse.bass as bass
import concourse.mybir as mybir
from concourse.bass_utils import run_bass_kernel_spmd

F32 = mybir.dt.float32
BF16 = mybir.dt.bfloat16
ALU = mybir.AluOpType
AF = mybir.ActivationFunctionType
AX = mybir.AxisListType

DEPTH = 4
EPS = 1e-6
NCORES = 8
TS = 2048
TP = 512
LCTX = 512


class _Rec:
    def __init__(self):
        self.call = None

    def __getattr__(self, name):
        def f(*a, **k):
            self.call = (name, a, k)
            return self
        return f


class Prog:
    ENGS = ("pe", "act", "dve", "pool", "sp")

    def __init__(self, nc):
        self.nc = nc
        self.ops = []
        self.reg = {}
        self.dma_cum = {}

    def add(self, eng, fn, reads=(), writes=(), dma=None, late_keys=None):
        writes = list(writes)
        for k in reads:
            if (k in ("pa", "pb", "pc") or (isinstance(k, tuple) and k[0] in ("s", "o"))) and k not in writes:
                writes.append(k)
        rec = _Rec()
        fn(rec)
        name_, a_, k_ = rec.call
        fn = (lambda name_=name_, a_=a_, k_=k_: lambda e: getattr(e, name_)(*a_, **k_))()
        i = len(self.ops)
        deps = {}
        strict = set()

        def note(d, k):
            if eng == "pe" and (late_keys is None or k not in late_keys):
                strict.add(d)

        for k in reads:
            e = self.reg.get(k)
            if e is not None and e[0] is not None:
                deps[e[0]] = "raw"
                note(e[0], k)
        for k in writes:
            e = self.reg.get(k)
            if e is not None:
                if e[0] is not None:
                    if e[0] not in deps:
                        if not (dma is not None and self.ops[e[0]]["dma"] is not None):
                            deps[e[0]] = "waw"
                    note(e[0], k)
                for r in e[1]:
                    if r not in deps:
                        deps[r] = "war"
                    note(r, k)
        deps.pop(i, None)
        for k in reads:
            self.reg.setdefault(k, [None, []])[1].append(i)
        for k in writes:
            self.reg[k] = [i, []]
        op = dict(eng=eng, fn=fn, deps=deps, dma=dma, cum=None, snap=None, strict=strict)
        if dma is not None:
            self.dma_cum[dma] = self.dma_cum.get(dma, 0) + 16
            op["cum"] = self.dma_cum[dma]
        op["snap"] = dict(self.dma_cum)
        self.ops.append(op)
        return i

    def emit(self, stack):
        nc = self.nc
        ops = self.ops

        def skip(op, dop, kind):
            if dop["dma"] is None and dop["eng"] == op["eng"] and op["dma"] is None:
                if op["eng"] == "pe":
                    return True
            return False

        need = set()
        for i, op in enumerate(ops):
            for d, kind in op["deps"].items():
                dop = ops[d]
                if dop["dma"] is not None or skip(op, dop, kind):
                    continue
                need.add(d)
        sig = {}
        cnt = {e: 0 for e in self.ENGS}
        for i, op in enumerate(ops):
            if op["dma"] is None and i in need:
                cnt[op["eng"]] += 1
                sig[i] = cnt[op["eng"]]
        sems = {e: stack.enter_context(nc.semaphore("s_" + e)) for e in self.ENGS}
        dsems = {n: stack.enter_context(nc.semaphore("d_" + n)) for n in self.dma_cum}
        per_eng = {e: [] for e in self.ENGS}
        seen = {e: {} for e in self.ENGS}
        for i, op in enumerate(ops):
            waits = {}
            for d, kind in op["deps"].items():
                dop = ops[d]
                if dop["dma"] is not None:
                    key = ("d", dop["dma"])
                    val = op["snap"][dop["dma"]]
                else:
                    if skip(op, dop, kind):
                        continue
                    key = ("e", dop["eng"])
                    val = sig[d]
                st_ = d in op["strict"]
                if key not in waits:
                    waits[key] = [val, st_]
                else:
                    waits[key][0] = max(waits[key][0], val)
                    waits[key][1] = waits[key][1] or st_
            wl = []
            sn = seen[op["eng"]]
            for key, (val, st_) in waits.items():
                if sn.get(key, 0) >= val:
                    continue
                sn[key] = val
                wl.append((sems[key[1]] if key[0] == "e" else dsems[key[1]], val, st_))
            per_eng[op["eng"]].append((i, op, wl))
        final_waits = [(dsems[n], v) for n, v in self.dma_cum.items()]
        self.n_attached = sum(1 for e_ in per_eng.values() for (_, o_, w_) in e_ if o_["dma"] is None and any(not x[2] for x in w_))
        final_eng = [(sems[e], cnt[e]) for e in self.ENGS if cnt[e] > 0]

        def run(eng_name, e):
            for i, op, wl in per_eng[eng_name]:
                attach = None
                if op["dma"] is None:
                    cands = [w for w in wl if not w[2]]
                    if cands:
                        attach = cands[-1]
                for w in wl:
                    if w is not attach:
                        e.wait_ge(w[0], w[1])
                ins = op["fn"](e)
                if attach is not None:
                    ins._wait_ge(attach[0], attach[1])
                if op["dma"] is not None:
                    ins.then_inc(dsems[op["dma"]], 16)
                elif i in sig:
                    ins.then_inc(sems[eng_name], 1)
            if eng_name == "sp":
                for s, v in final_waits:
                    e.wait_ge(s, v)
                for s, v in final_eng:
                    e.wait_ge(s, v)

        with nc.Block() as block:
            @block.tensor
            def _(e):
                run("pe", e)

            @block.scalar
            def _(e):
                run("act", e)

            @block.vector
            def _(e):
                run("dve", e)

            @block.gpsimd
            def _(e):
                run("pool", e)

            @block.sync
            def _(e):
                run("sp", e)


def lam_init_of(l):
    return 0.8 - 0.6 * math.exp(-0.3 * l)


def build_program():
    nc = bass.Bass("TRN2", target_bir_lowering=False)

    def din(name, shape):
        return nc.dram_tensor(name, list(shape), F32, kind="ExternalInput").ap()

    def dout(name, shape):
        return nc.dram_tensor(name, list(shape), F32, kind="ExternalOutput").ap()

    xpT_d = din("xpT", [1024, TP])
    xsT_d = din("xsT", [1024, TS])
    condT_d = din("condT", [1024, 2])
    w_ada_d = din("w_ada", [DEPTH, 4, 1024, 768])
    adabT_d = din("adabT", [128, DEPTH * 24])
    normgT_d = din("normgT", [128, DEPTH * 8])
    fnormgT_d = din("fnormgT", [128, 8])
    w_A_d = din("w_A", [DEPTH, 2, 1024, 768])
    w_B_d = din("w_B", [DEPTH, 2, 1024, 512])
    w_C_d = din("w_C", [DEPTH, 2, 1024, 768])
    w_D_d = din("w_D", [DEPTH, 2, 1024, 512])
    w_out_d = din("w_out", [DEPTH, 1024, 1024])
    w_uq_d = din("w_uq", [DEPTH, 2, 256, 512])
    w_ukv_d = din("w_ukv", [DEPTH, 2, 128, 256])
    qng_d = din("qng", [128, DEPTH * 2])
    kvng_d = din("kvng", [128, DEPTH])
    lam_d = din("lam", [DEPTH, 128])
    subgc_d = din("subgc", [128, DEPTH])
    convw_d = din("convw", [128, DEPTH * 2 * 3])
    ebias_d = din("ebias", [DEPTH, 128, 4 * 16 * 64])
    cakT_d = din("cakT", [DEPTH, 2, 128, LCTX])
    cav_d = din("cav", [DEPTH, LCTX, 256])
    cckvT_d = din("cckvT", [DEPTH, 128, LCTX])
    ckpeT_d = din("ckpeT", [DEPTH, 32, LCTX])
    cdkT_d = din("cdkT", [DEPTH, 2, 128, LCTX])
    cdv_d = din("cdv", [DEPTH, LCTX, 256])
    tabA_d = din("tabA", [128, 200])
    tabC_d = din("tabC", [96, 192])

    ypT_d = dout("ypT", [1024, TP])
    ysT_d = dout("ysT", [1024, TS])
    sakT_d = dout("sakT", [DEPTH, 256, TP])
    sav_d = dout("sav", [DEPTH, TP, 256])
    sckvT_d = dout("sckvT", [DEPTH, 128, TP])
    skpeT_d = dout("skpeT", [DEPTH, 32, TP])
    sdkT_d = dout("sdkT", [DEPTH, 256, TP])
    sdv_d = dout("sdv", [DEPTH, TP, 256])

    with ExitStack() as st:
        def sb(name, shape, dt):
            return st.enter_context(nc.sbuf_tensor(name, list(shape), dt))

        def psum(name, shape, dt):
            return st.enter_context(nc.psum_tensor(name, list(shape), dt))

        P = Prog(nc)
        add = P.add

        xT = {"S": sb("xTs", [128, 8, TS], F32), "P": sb("xTp", [128, 8, TP], F32)}
        hT = {"S": sb("hTs", [128, 8, TS], BF16), "P": sb("hTp", [128, 8, TP], BF16)}
        NT = {"S": TS, "P": TP}
        NB = {"S": 4, "P": 1}
        wslot = [sb("wslot0", [128, 8, 768], BF16), sb("wslot1", [128, 8, 768], BF16)]
        woslot = [sb("wo0", [128, 1024], BF16), sb("wo1", [128, 1024], BF16)]
        KT = sb("KT", [128, 2, LCTX + TS], BF16)
        VA = sb("VA", [128, 20, 2, 128], BF16)
        SH = sb("SH", [128, 4096], BF16)
        CKV = SH[:, 0:LCTX + TS]
        qb = [sb("qb%d" % i, [128, 512], BF16) for i in range(2)]
        HM = sb("HM", [128, 2, 64], BF16)
        PT = [sb("PT%d" % i, [128, 512], BF16) for i in range(3)]
        PTL = [sb("PTL%d" % i, [128, 5, 64], BF16) for i in range(2)]
        zs = sb("zs", [128, 512], F32)
        t2b = sb("t2b", [128, 512], F32)
        rcp = sb("rcp", [128, 516], F32)
        blockones = sb("blockones", [128, 128], BF16)
        yTb = sb("yTb", [128, 512], BF16)
        sqb = [sb("sq0", [128, 512], BF16), sb("sq1", [128, 512], BF16)]
        qb = qb + sqb
        qkeys = [("qb", 0), ("qb", 1), ("sq", 0), ("sq", 1)]
        stage = [sb("stage0", [128, 512], F32), sb("stage1", [128, 512], F32)]
        tmpf = stage
        of1 = sb("of1", [128, 512], F32)
        rstd_t = of1
        of2 = sb("of2", [128, 512], F32)
        small = sb("small", [128, 64], F32)
        mods = [sb("mod0", [128, 24, 2], F32), sb("mod1", [128, 24, 2], F32)]
        gmod = sb("gmod", [128, 8, 2], F32)
        cond_sb = sb("cond_sb", [128, 8, 2], F32)
        csil = sb("csil", [128, 8, 2], BF16)
        adab_sb = sb("adab_sb", [128, DEPTH, 24], F32)
        normg_sb = sb("normg_sb", [128, DEPTH, 8], F32)
        fnormg_sb = sb("fnormg_sb", [128, 8], F32)
        qng_sb = sb("qng_sb", [128, DEPTH, 2], F32)
        kvng_sb = sb("kvng_sb", [128, DEPTH], F32)
        lam_sb = sb("lam_sb", [128, 128], F32)
        subgc_sb = sb("subgc_sb", [128, DEPTH], F32)
        subgs = sb("subgs", [128, 1], F32)
        lamv = sb("lamv", [128, 8], F32)
        convw_sb = sb("convw_sb", [128, DEPTH, 2, 3], F32)
        tabA = sb("tabA_sb", [128, 200], F32)
        tabC = sb("tabC_sb", [96, 192], F32)
        E8 = SH[:, 0:4096].rearrange("p (h j c) -> p h j c", h=4, j=16)
        ones_bf = sb("ones_bf", [128, 128], BF16)
        ident = sb("ident", [128, 128], BF16)
        wuq = sb("wuq", [128, 2, 512], BF16)
        wukv = sb("wukv", [128, 256], BF16)
        G_S = KT[:].rearrange("p a b -> p (a b)").bitcast(F32)[:, 0:TS + 2]
        G_P = rcp[:, 0:516].rearrange("p (s t) -> p s t", s=2)
        BBZ = SH[:, 0:TS]

        pa = psum("pa", [128, 512], F32)
        pb = psum("pb", [128, 512], F32)
        pc = psum("pc", [128, 512], F32)
        sps = [psum("s%d" % i, [128, 512], F32) for i in range(3)]
        ops_ = [psum("oA", [128, 512], F32), psum("oB", [128, 512], F32)]
        pcbf = pc[:].bitcast(BF16)
        proj_banks = [(pa, "pa"), (pb, "pb"), (pc, "pc")]
        pp_banks = [(pa, "pa"), (pb, "pb"), (pc, "pc")]
        op_banks = [(sps[0], ("s", 0)), (sps[1], ("s", 1)), (sps[2], ("s", 2)), (pc, "pc"), (pb, "pb"), (pa, "pa")]
        banksets = [[(pa, "pa"), (pb, "pb"), (pc, "pc")], [(sps[0], ("s", 0)), (sps[1], ("s", 1)), (sps[2], ("s", 2))]]

        cnt = {"opj": 0, "ppj": 0, "s": 0, "pt": 0, "o": 0, "sq": 0, "tmp": 0, "stage": 0, "ptl": 0, "pj": 0}

        def rot(name, n):
            v = cnt[name] % n
            cnt[name] += 1
            return v

        add("pool", lambda e: e.memset(ones_bf[:], 1.0), writes=["ones"])
        add("pool", lambda e: e.memset(ident[:], 0.0), writes=["ident"])
        add("pool", lambda e: e.affine_select(out=ident[:], in_=ident[:], compare_op=ALU.not_equal, fill=1.0,
                                              base=0, pattern=[[-1, 128]], channel_multiplier=1),
            reads=["ident"], writes=["ident"])
        add("pool", lambda e: e.memset(VA[:], 1.0), writes=["VA"])
        add("pool", lambda e: e.memset(HM[:], 0.0), writes=["HM"])
        add("pool", lambda e: e.memset(HM[0:64, 0, :], -240000.0), writes=["HM"])
        add("pool", lambda e: e.memset(HM[64:128, 1, :], -240000.0), writes=["HM"])
        add("pool", lambda e: e.memset(blockones[:], 0.0), writes=["blockones"])
        add("pool", lambda e: e.memset(blockones[0:64, 0:64], 1.0), writes=["blockones"])
        add("pool", lambda e: e.memset(blockones[64:128, 64:128], 1.0), writes=["blockones"])

        def ld(dst, src, key, sem="misc", eng="sp"):
            add(eng, lambda e: e.dma_start(out=dst, in_=src), writes=[key], dma=sem)

        ld(cond_sb[:], condT_d.rearrange("(k p) c -> p k c", p=128), "cond")
        ld(adab_sb[:], adabT_d.rearrange("p (l c) -> p l c", l=DEPTH), "adab")
        ld(normg_sb[:], normgT_d.rearrange("p (l c) -> p l c", l=DEPTH), "normg")
        ld(fnormg_sb[:], fnormgT_d, "fnormg")
        ld(qng_sb[:], qng_d.rearrange("p (l c) -> p l c", l=DEPTH), "qng")
        ld(kvng_sb[:], kvng_d, "kvng")
        ld(convw_sb[:], convw_d.rearrange("p (l h c) -> p l h c", l=DEPTH, h=2), "convw")
        ld(tabA[:], tabA_d, "tabA")
        ld(subgc_sb[:], subgc_d, "subgc")
        ld(tabC[:], tabC_d, "tabC")
        for k in range(8):
            ld(xT["P"][:, k, :], xpT_d[k * 128:(k + 1) * 128, :], ("x", "P", 0), sem="xin")
            for b in range(4):
                ld(xT["S"][:, k, b * 512:(b + 1) * 512], xsT_d[k * 128:(k + 1) * 128, b * 512:(b + 1) * 512],
                   ("x", "S", b), sem="xin")

        add("act", lambda e: e.activation(out=small[:, 0:16], in_=cond_sb[:].rearrange("p k c -> p (k c)"),
                                          func=AF.Exp, scale=-1.0), reads=["cond"], writes=["small"])
        add("dve", lambda e: e.tensor_scalar_add(out=small[:, 0:16], in0=small[:, 0:16], scalar1=1.0),
            reads=["small"], writes=["small"])
        add("dve", lambda e: e.reciprocal(out=small[:, 0:16], in_=small[:, 0:16]), reads=["small"], writes=["small"])
        add("dve", lambda e: e.tensor_tensor(out=csil[:].rearrange("p k c -> p (k c)"), in0=small[:, 0:16],
                                             in1=cond_sb[:].rearrange("p k c -> p (k c)"), op=ALU.mult),
            reads=["small", "cond"], writes=["csil"])

        items = []
        for j in range(4):
            items.append(("ada", 0, j))
        for l in range(DEPTH):
            nxt = [("ada", l + 1, j) for j in range(4)] if l + 1 < DEPTH else []
            order = [("A", l, 0), ("A", l, 1)] + nxt[0:1] + [("B", l, 0), ("B", l, 1)] + nxt[1:2] + \
                    [("C", l, 0), ("C", l, 1)] + nxt[2:3] + [("D", l, 0)] + nxt[3:4] + [("D", l, 1)]
            items.extend(order)
        WIDTH = {"ada": 768, "A": 768, "B": 512, "C": 768, "D": 512}
        WSRC = {"ada": w_ada_d, "A": w_A_d, "B": w_B_d, "C": w_C_d, "D": w_D_d}
        GOFF = {"A": 0, "B": 256, "C": 512, "D": 768}

        def load_item(idx):
            kind, l, j = items[idx]
            s = idx % 2
            wd = WIDTH[kind]
            src = WSRC[kind][l, j]
            for k in range(8):
                add("pool", (lambda k=k: lambda e: e.dma_start(out=wslot[s][:, k, 0:wd],
                                                               in_=src[k * 128:(k + 1) * 128, :]))(),
                    writes=[("w", s, k)], dma="w%d" % s)
            if kind != "ada":
                r0 = GOFF[kind] + j * 128
                add("pool", lambda e: e.dma_start(out=woslot[s][:], in_=w_out_d[l, r0:r0 + 128, :]),
                    writes=[("wo", s)], dma="wo%d" % s)

        def wkeys(s):
            return [("w", s, k) for k in range(8)]

        def fm(s, c0, M, sname, tok0, n, ps, pskey, extra_reads=()):
            for k in range(8):
                add("pe", (lambda k=k: lambda e: e.matmul(ps[0:M, 0:n], wslot[s][:, k, c0:c0 + M],
                                                          hT[sname][:, k, tok0:tok0 + n],
                                                          start=(k == 0), stop=(k == 7)))(),
                    reads=[("w", s, k), ("h", sname, tok0 // 512)] + list(extra_reads), writes=[pskey])

        def tm(s, c0, N, sname, tok0, ps_ap, pskey):
            for k in range(8):
                add("pe", (lambda k=k: lambda e: e.matmul(ps_ap, hT[sname][:, k, tok0:tok0 + 128],
                                                          wslot[s][:, k, c0:c0 + N],
                                                          start=(k == 0), stop=(k == 7)))(),
                    reads=[("w", s, k), ("h", sname, tok0 // 512)], writes=[pskey])

        def rstd_from_ps(ps, n_feat, rkey="of1"):
            add("act", lambda e: e.activation(out=rstd_t[:], in_=ps[:], func=AF.Ln, scale=1.0 / n_feat,
                                              bias=eps_t[:, 0:1]),
                reads=[rkey + "_ps", "eps"], writes=["of1"])
            add("act", lambda e: e.activation(out=rstd_t[:], in_=rstd_t[:], func=AF.Exp, scale=-0.5),
                reads=["of1"], writes=["of1"])

        eps_t = sb("eps_t", [128, 1], F32)
        add("pool", lambda e: e.memset(eps_t[:], EPS), writes=["eps"])
        one_t = sb("one_t", [128, 1], F32)
        add("pool", lambda e: e.memset(one_t[:], 1.0), writes=["one"])

        def silu_from_ps(ps_ap, pskey, out_ap, outkey, shape_n):
            sc = of2[:, 0:shape_n]
            add("act", lambda e: e.activation(out=sc, in_=ps_ap, func=AF.Exp, scale=-1.0), reads=[pskey], writes=["of2"])
            add("act", lambda e: e.activation(out=sc, in_=sc, func=AF.Ln, bias=one_t[:, 0:1], scale=1.0), reads=["of2", "one"], writes=["of2"])
            add("act", lambda e: e.activation(out=sc, in_=sc, func=AF.Exp, scale=-1.0), reads=["of2"], writes=["of2"])
            add("dve", lambda e: e.tensor_tensor(out=out_ap, in0=ps_ap, in1=sc, op=ALU.mult),
                reads=[pskey, "of2"], writes=[outkey])

        def store(dst_ap, src_ap, srckey, sem=None):
            sem = "st%d" % srckey[1]
            add("sp", lambda e: e.dma_start(out=dst_ap, in_=src_ap), reads=[srckey], dma=sem)

        def rope(ps_x, xkey, ps_p, pkey, tab, rows, blk, out_ap, outkey, nrow=128, tkey="tabA", scratch=None):
            r0 = blk * 8
            CR = tab[rows, r0:r0 + 8].unsqueeze(2).broadcast_to([nrow, 8, 64])
            SR = tab[rows, 32 + r0:32 + r0 + 8].unsqueeze(2).broadcast_to([nrow, 8, 64])
            CC = tab[rows, 64:128].unsqueeze(1).broadcast_to([nrow, 8, 64])
            SC = tab[rows, 128:192].unsqueeze(1).broadcast_to([nrow, 8, 64])
            v = lambda ap: ap.rearrange("p (r c) -> p r c", c=64)
            (s1_, k1_), (s2_, k2_) = scratch if scratch is not None else ((of1, "of1"), (of2, "of2"))
            t1 = s1_[rows, 0:512]
            t2 = s2_[rows, 0:512]
            add("dve", lambda e: e.tensor_tensor(out=v(t1), in0=v(ps_x), in1=CR, op=ALU.mult), reads=[xkey, tkey], writes=[k1_])
            add("dve", lambda e: e.tensor_tensor(out=v(t1), in0=v(t1), in1=CC, op=ALU.mult), reads=[k1_, tkey], writes=[k1_])
            add("dve", lambda e: e.tensor_tensor(out=v(t2), in0=v(ps_p), in1=SR, op=ALU.mult), reads=[pkey, tkey], writes=[k2_])
            add("dve", lambda e: e.tensor_tensor(out=v(t2), in0=v(t2), in1=SC, op=ALU.mult), reads=[k2_, tkey], writes=[k2_])
            add("dve", lambda e: e.tensor_tensor(out=out_ap, in0=t1, in1=t2, op=ALU.add), reads=[k1_, k2_], writes=[outkey])

        def ada_piece(idx):
            kind, l, j = items[idx]
            s = idx % 2
            for m in range(6):
                for k in range(8):
                    add("pe", (lambda m=m, k=k: lambda e: e.matmul(pa[:, 2 * m:2 * m + 2], wslot[s][:, k, m * 128:(m + 1) * 128],
                                                                   csil[:, k, :], start=(k == 0), stop=(k == 7)))(),
                        reads=[("w", s, k), "csil"], writes=["pa"])
            mod = mods[l % 2]
            add("dve", lambda e: e.tensor_tensor(out=mod[:, 6 * j:6 * j + 6, :],
                                                 in0=pa[:, 0:12].rearrange("p (m c) -> p m c", c=2),
                                                 in1=adab_sb[:, l, 6 * j:6 * j + 6].unsqueeze(2).broadcast_to([128, 6, 2]),
                                                 op=ALU.add),
                reads=["pa", "adab"], writes=[("mod", l % 2, j)])

        def layer_prep(l):
            mod = mods[l % 2]
            add("dve", lambda e: e.tensor_scalar_add(out=gmod[:], in0=mod[:, 8:16, :], scalar1=1.0),
                reads=[("mod", l % 2, jj) for jj in range(4)], writes=["gmod"])
            add("dve", lambda e: e.tensor_tensor(out=gmod[:], in0=gmod[:],
                                                 in1=normg_sb[:, l, :].unsqueeze(2).broadcast_to([128, 8, 2]), op=ALU.mult),
                reads=["gmod", "normg"], writes=["gmod"])
            li = lam_init_of(l)
            ld(lam_sb[:], lam_d[l:l + 1, :].partition_broadcast(128), "lam", sem="lamld")
            add("dve", lambda e: e.tensor_tensor(out=small[:, 0:32], in0=lam_sb[:, 0:32], in1=lam_sb[:, 32:64], op=ALU.mult),
                reads=["lam"], writes=["small"])
            add("dve", lambda e: e.tensor_tensor(out=small[:, 32:64], in0=lam_sb[:, 64:96], in1=lam_sb[:, 96:128], op=ALU.mult),
                reads=["lam", "small"], writes=["small"])
            add("dve", lambda e: e.reduce_sum(out=lamv[:, 0:2], in_=small[:, 0:64].rearrange("p (a b) -> p a b", b=32), axis=AX.X),
                reads=["small"], writes=["lamv"])
            add("act", lambda e: e.activation(out=lamv[:, 2:4], in_=lamv[:, 0:2], func=AF.Exp), reads=["lamv"], writes=["lamv2"])
            add("dve", lambda e: e.tensor_tensor(out=lamv[:, 4:5], in0=lamv[:, 3:4], in1=lamv[:, 2:3], op=ALU.subtract),
                reads=["lamv2"], writes=["lamv3"])
            add("dve", lambda e: e.tensor_scalar_add(out=lamv[:, 5:6], in0=lamv[:, 4:5], scalar1=-li), reads=["lamv3"], writes=["neglam"])
            add("dve", lambda e: e.tensor_scalar_mul(out=subgs[:], in0=subgc_sb[:, l:l + 1], scalar1=(1.0 - li)),
                reads=["subgc"], writes=["subgs"])

        def norm_block(l, sname, b, final=False):
            ci = 1 if sname == "S" else 0
            x = xT[sname]
            tk = slice(b * 512, (b + 1) * 512)
            for k in range(8):
                q = rot("sq", 2)
                add("act", (lambda k=k, q=q: lambda e: e.activation(out=sqb[q][:], in_=x[:, k, tk], func=AF.Square))(),
                    reads=[("x", sname, b)], writes=[("sq", q)])
                add("pe", (lambda k=k, q=q: lambda e: e.matmul(pa[:], ones_bf[:], sqb[q][:], start=(k == 0), stop=(k == 7)))(),
                    reads=["ones", ("sq", q)], writes=["pa"])
            add("act", lambda e: e.activation(out=rstd_t[:], in_=pa[:], func=AF.Ln, scale=1.0 / 1024, bias=eps_t[:, 0:1]),
                reads=["pa", "eps"], writes=["of1"])
            add("act", lambda e: e.activation(out=rstd_t[:], in_=rstd_t[:], func=AF.Exp, scale=-0.5), reads=["of1"], writes=["of1"])
            for k in range(8):
                if final:
                    q = rot("stage", 2)
                    add("dve", (lambda k=k, q=q: lambda e: e.scalar_tensor_tensor(out=stage[q][:], in0=x[:, k, tk], scalar=fnormg_sb[:, k:k + 1],
                                                                                   in1=rstd_t[:], op0=ALU.mult, op1=ALU.mult))(),
                        reads=[("x", sname, b), "of1", "fnormg"], writes=[("stage", q)])
                    dst = (ysT_d if sname == "S" else ypT_d)[k * 128:(k + 1) * 128, tk]
                    store(dst, stage[q][:], ("stage", q))
                else:
                    q = rot("tmp", 2)
                    add("dve", (lambda k=k, q=q: lambda e: e.scalar_tensor_tensor(out=tmpf[q][:], in0=x[:, k, tk], scalar=gmod[:, k, ci:ci + 1],
                                                                                   in1=rstd_t[:], op0=ALU.mult, op1=ALU.mult))(),
                        reads=[("x", sname, b), "of1", "gmod"], writes=[("stage", q)])
                    add("act", (lambda k=k, q=q: lambda e: e.activation(out=hT[sname][:, k, tk], in_=tmpf[q][:], func=AF.Identity,
                                                                         bias=mods[l % 2][:, k, ci:ci + 1], scale=1.0))(),
                        reads=[("stage", q)] + [("mod", l % 2, jj) for jj in range(4)], writes=[("h", sname, b)])

        def attend(lhs_of_kt, rhs_q, nq, kts, v_of_kt, scale, OT, okey, col0, kreads, qkey, first=True, last=True):
            n = len(kts)
            slots = [(rot("s", 3), rot("pt", 3)) for _ in range(n)]

            def smm(ii):
                si, pi = slots[ii]
                S = sps[si]
                kt = kts[ii]
                add("pe", lambda e: e.matmul(S[:, 0:nq], lhs_of_kt(kt), rhs_q, start=True, stop=True),
                    reads=list(kreads) + [qkey], writes=[("s", si)], late_keys=(("s", si), qkey))

            for ii in range(min(2, n)):
                smm(ii)
            for ii, kt in enumerate(kts):
                si, pi = slots[ii]
                S = sps[si]
                add("act", lambda e: e.activation(out=PT[pi][:, 0:nq], in_=S[:, 0:nq], func=AF.Exp, scale=scale),
                    reads=[("s", si)], writes=[("pt", pi)])
                if ii + 2 < n:
                    smm(ii + 2)
                add("pe", lambda e: e.matmul(OT[:, col0:col0 + nq], v_of_kt(kt), PT[pi][:, 0:nq],
                                             start=(first and ii == 0), stop=(last and ii == n - 1)),
                    reads=[("pt", pi), "VA"], writes=[okey], late_keys=(("pt", pi), okey))

        def outproj(s, l, sname, b, banks=None):
            ci = 1 if sname == "S" else 0
            tk = slice(b * 512, (b + 1) * 512)
            for cc in range(8):
                bank, bkey = op_banks[rot("opj", 6)] if banks is None else banks[rot("ppj", len(banks))]
                add("pe", lambda e: e.matmul(bank[:], woslot[s][:, cc * 128:(cc + 1) * 128], yTb[:], start=True, stop=True),
                    reads=[("wo", s), "yTb"], writes=[bkey])
                add("dve", lambda e: e.scalar_tensor_tensor(
                    out=xT[sname][:, cc, tk], in0=bank[:], scalar=mods[l % 2][:, 16 + cc, ci:ci + 1], in1=xT[sname][:, cc, tk],
                    op0=ALU.mult, op1=ALU.add),
                    reads=[bkey, ("mod", l % 2, 2), ("mod", l % 2, 3), ("x", sname, b)], writes=[("x", sname, b)])

        def z_block(s, c0, sname, b):
            fm(s, c0, 128, sname, b * 512, 512, pc, "pc")
            silu_from_ps(pc[:], "pc", zs[:], "zs", 512)

        def v_put(src_ps, kt0, nt, eng="act", pkey="pc"):
            v4 = src_ps.rearrange("p (t h d) -> p t h d", t=nt, h=2)
            for h in range(2):
                if eng == "act":
                    add("act", lambda e: e.activation(out=VA[:, kt0:kt0 + nt, h, h * 64:(h + 1) * 64], in_=v4[:, :, h, :], func=AF.Copy),
                        reads=[pkey], writes=["VA"])
                else:
                    add("dve", lambda e: e.tensor_copy(out=VA[:, kt0:kt0 + nt, h, h * 64:(h + 1) * 64], in_=v4[:, :, h, :]),
                        reads=[pkey], writes=["VA"])

        def v_cache(src_d, l, hp):
            for t in range(4):
                for h in range(2):
                    add("pool", lambda e: e.dma_start(
                        out=VA[:, t, h, h * 64:(h + 1) * 64],
                        in_=src_d[l, t * 128:(t + 1) * 128, hp * 128 + h * 64:hp * 128 + (h + 1) * 64]),
                        writes=["VA"], dma="kv")

        def v_block(s, c0, sname, b, l, hp, state_d=None, bank=None):
            pc_, pck = bank if bank is not None else (pc, "pc")
            base = 4 if sname == "S" else 0
            for j in range(4):
                tm(s, c0, 128, sname, b * 512 + j * 128, pc_[:, j * 128:(j + 1) * 128], pck)
            v_put(pc_[:], base + b * 4, 4, pkey=pck)
            if state_d is not None:
                q = rot("stage", 2)
                add("dve", lambda e: e.tensor_copy(out=stage[q][:], in_=pc_[:]), reads=[pck], writes=[("stage", q)])
                for j in range(4):
                    store(state_d[l, j * 128:(j + 1) * 128, hp * 128:(hp + 1) * 128], stage[q][:, j * 128:(j + 1) * 128], ("stage", q))

        def den_rows(hh):
            return slice(64, 128) if hh == 0 else slice(0, 64)

        def normalize_head(OT, okey, hh, dst, dkey, on_act=False):
            hs = slice(hh * 64, (hh + 1) * 64)
            if on_act:
                add("act", lambda e: e.activation(out=rcp[hs, 0:512], in_=OT[den_rows(hh), :], func=AF.Ln), reads=[okey], writes=["rcp"])
                add("act", lambda e: e.activation(out=rcp[hs, 0:512], in_=rcp[hs, 0:512], func=AF.Exp, scale=-1.0),
                    reads=["rcp"], writes=["rcp"])
            else:
                add("dve", lambda e: e.reciprocal(out=rcp[hs, 0:512], in_=OT[den_rows(hh), :]), reads=[okey], writes=["rcp"])
            add("dve", lambda e: e.tensor_tensor(out=dst[hs, :], in0=OT[hs, :], in1=rcp[hs, 0:512], op=ALU.mult),
                reads=[okey, "rcp"], writes=[dkey])

        def attend_set(sname, klhs, qrhs, hh, scale, OT, okey, qkey):
            if sname == "S":
                attend(klhs, qrhs(0, 512), 512, list(range(20)), lambda kt: VA[:, kt, hh, :], scale, OT, okey, 0, ["KT"], qkey)
            else:
                for sq_ in range(2):
                    attend(klhs, qrhs(sq_ * 256, 256), 256, [2 * sq_, 2 * sq_ + 1], lambda kt: VA[:, kt, hh, :], scale, OT, okey,
                           sq_ * 256, ["KT"], qkey, first=(sq_ == 0), last=(sq_ == 1))

        def phase_A(idx):
            kind, l, hp = items[idx]
            s = idx % 2
            sc = 32 ** -0.5
            for sname in ("P", "S"):
                L = LCTX if sname == "S" else 0
                if sname == "S":
                    add("pool", lambda e: e.dma_start(out=KT[:, 0, 0:LCTX], in_=cakT_d[l, hp]), writes=["KT"], dma="kv")
                    v_cache(cav_d, l, hp)
                for b in range(NB[sname]):
                    (A_, Ak), (B_, Bk), (C_, Ck) = banksets[b % 2]
                    fm(s, 256, 128, sname, b * 512, 512, A_, Ak)
                    if sname == "S":
                        fm(s, 384, 128, sname, b * 512, 512, B_, Bk)
                        rope(A_[:], Ak, B_[:], Bk, tabA, slice(0, 128), b, KT[:, 0, L + b * 512:L + (b + 1) * 512], "KT")
                    else:
                        q = rot("stage", 2)
                        add("act", lambda e: e.activation(out=stage[q][:], in_=A_[:], func=AF.Copy), reads=[Ak], writes=[("stage", q)])
                        add("dve", lambda e: e.tensor_copy(out=KT[:, 0, 0:512], in_=A_[:]), reads=[Ak], writes=["KT"])
                        store(sakT_d[l, hp * 128:(hp + 1) * 128, :], stage[q][:], ("stage", q))
                    v_block(s, 512, sname, b, l, hp, state_d=(sav_d if sname == "P" else None), bank=(C_, Ck))
                def prologue(b):
                    fm(s, 0, 128, sname, b * 512, 512, pa, "pa")
                    if sname == "S":
                        fm(s, 128, 128, sname, b * 512, 512, pb, "pb")
                        rope(pa[:], "pa", pb[:], "pb", tabA, slice(0, 128), b, tmpf[0][:], ("stage", 0),
                             scratch=((rcp, "rcp"), (of2, "of2")))
                        src, skey = tmpf[0][:], ("stage", 0)
                    else:
                        src, skey = pa[:], "pa"
                    for qi in range(4):
                        add("dve", lambda e: e.tensor_scalar_mul(out=qb[qi][:], in0=src, scalar1=tabA[:, 194 + qi:195 + qi]),
                            reads=[skey, "tabA"], writes=[qkeys[qi]])

                def post(b):
                    z_block(s, 640, sname, b)
                    add("dve", lambda e: e.scalar_tensor_tensor(out=of1[:], in0=t2b[:], scalar=lamv[:, 5:6], in1=of1[:],
                                                                op0=ALU.mult, op1=ALU.add),
                        reads=["t2b", "neglam", "of1"], writes=["of1"])
                    pq = rot("pt", 3)
                    add("act", lambda e: e.activation(out=PT[pq][:], in_=of1[:], func=AF.Square), reads=["of1"], writes=[("pt", pq)])
                    add("pe", lambda e: e.matmul(pa[:], blockones[:], PT[pq][:], start=True, stop=True),
                        reads=["blockones", ("pt", pq)], writes=["pa"])
                    add("act", lambda e: e.activation(out=t2b[:], in_=pa[:], func=AF.Ln, scale=1.0 / 64, bias=eps_t[:, 0:1]),
                        reads=["pa", "eps"], writes=["t2b"])
                    add("act", lambda e: e.activation(out=t2b[:], in_=t2b[:], func=AF.Exp, scale=-0.5), reads=["t2b"], writes=["t2b"])
                    add("dve", lambda e: e.scalar_tensor_tensor(out=of1[:], in0=of1[:], scalar=subgs[:, 0:1], in1=t2b[:],
                                                                op0=ALU.mult, op1=ALU.mult),
                        reads=["of1", "subgs", "t2b"], writes=["of1"])
                    add("dve", lambda e: e.tensor_tensor(out=yTb[:], in0=of1[:], in1=zs[:], op=ALU.mult), reads=["of1", "zs"], writes=["yTb"])
                    outproj(s, l, sname, b, banks=pp_banks)

                tbuf = [of1, t2b]
                tkey = ["of1", "t2b"]
                pending = None
                prologue(0)
                for b in range(NB[sname]):
                    first = True
                    for hh in range(2):
                        for c in range(2):
                            oi = rot("o", 2)
                            OT = ops_[oi]
                            qi = hh * 2 + c
                            attend_set(sname, lambda kt: KT[:, 0, kt * 128:(kt + 1) * 128],
                                       lambda q0, n, qi=qi: qb[qi][:, q0:q0 + n], hh, sc, OT, ("o", oi), qkeys[qi])
                            if first and pending is not None:
                                post(pending)
                                pending = None
                            first = False
                            normalize_head(OT, ("o", oi), hh, tbuf[c], tkey[c], on_act=(hh == 1 and c == 1))
                    if b + 1 < NB[sname]:
                        prologue(b + 1)
                    pending = b
                post(pending)

        def phase_C(idx):
            kind, l, hp = items[idx]
            s = idx % 2
            sc = 96 ** -0.5
            add("pool", lambda e: e.dma_start(out=wuq[:], in_=w_uq_d[l, hp].rearrange("(i p) c -> p i c", p=128)),
                writes=["wuq"], dma="wsm")
            add("pool", lambda e: e.dma_start(out=wukv[:], in_=w_ukv_d[l, hp]), writes=["wukv"], dma="wsm")
            add("dve", lambda e: e.memset(KT[96:128, :, :], 0.0), writes=["KT"])
            for j in range(2):
                add("dve", lambda e: e.memset(qb[j][96:128, :], 0.0), writes=[("qb", j)])
            for sname in ("P", "S"):
                L = LCTX if sname == "S" else 0
                if sname == "S":
                    add("pool", lambda e: e.dma_start(out=CKV[:, 0:LCTX], in_=cckvT_d[l]), writes=["SH"], dma="kv")
                    for j in range(2):
                        add("pool", lambda e: e.dma_start(out=KT[64:96, j, 0:LCTX], in_=ckpeT_d[l]), writes=["KT"], dma="kv")
                for b in range(NB[sname]):
                    tk = slice(L + b * 512, L + (b + 1) * 512)
                    (A_, Ak), (B_, Bk), (C_, Ck) = banksets[b % 2]
                    fm(s, 256, 128, sname, b * 512, 512, A_, Ak)
                    q = rot("sq", 2)
                    add("act", lambda e: e.activation(out=sqb[q][:], in_=A_[:], func=AF.Square), reads=[Ak], writes=[("sq", q)])
                    add("pe", lambda e: e.matmul(B_[:], ones_bf[:], sqb[q][:], start=True, stop=True),
                        reads=["ones", ("sq", q)], writes=[Bk])
                    add("act", lambda e: e.activation(out=rstd_t[:], in_=B_[:], func=AF.Ln, scale=1.0 / 128, bias=eps_t[:, 0:1]),
                        reads=[Bk, "eps"], writes=["of1"])
                    add("act", lambda e: e.activation(out=rstd_t[:], in_=rstd_t[:], func=AF.Exp, scale=-0.5), reads=["of1"], writes=["of1"])
                    qq = rot("stage", 2)
                    add("dve", lambda e: e.scalar_tensor_tensor(out=stage[qq][:], in0=A_[:], scalar=kvng_sb[:, l:l + 1], in1=rstd_t[:],
                                                                op0=ALU.mult, op1=ALU.mult),
                        reads=[Ak, "kvng", "of1"], writes=[("stage", qq)])
                    add("act", lambda e: e.activation(out=CKV[:, tk], in_=stage[qq][:], func=AF.Copy),
                        reads=[("stage", qq)], writes=["SH"])
                    if sname == "P" and hp == 0:
                        store(sckvT_d[l], stage[qq][:], ("stage", qq))
                    fm(s, 384, 128, sname, b * 512, 512, C_, Ck)
                    if sname == "S":
                        fm(s, 512, 128, sname, b * 512, 512, B_, Bk)
                        rope(C_[64:96, :], Ck, B_[64:96, :], Bk, tabC, slice(64, 96), b, KT[64:96, 0, tk], "KT", nrow=32, tkey="tabC")
                        add("pool", lambda e: e.tensor_copy(out=KT[64:96, 1, tk], in_=KT[64:96, 0, tk]), reads=["KT"], writes=["KT"])
                    else:
                        qq2 = rot("stage", 2)
                        add("dve", lambda e: e.tensor_copy(out=stage[qq2][64:96, :], in_=C_[64:96, :]),
                            reads=[Ck], writes=[("stage", qq2)])
                        for j in range(2):
                            add("act", lambda e: e.activation(out=KT[64:96, j, tk], in_=C_[64:96, :], func=AF.Copy),
                                reads=[Ck], writes=["KT"])
                        if hp == 0:
                            store(skpeT_d[l], stage[qq2][64:96, :], ("stage", qq2))
                nkc = (L + NT[sname]) // 512
                for kc in range(nkc):
                    bank, bkey = proj_banks[rot("pj", 2)]
                    add("pe", lambda e: e.matmul(bank[:], wukv[:, 0:128], CKV[:, kc * 512:(kc + 1) * 512], start=True, stop=True),
                        reads=["wukv", "SH"], writes=[bkey])
                    for j in range(2):
                        add("act", lambda e: e.activation(out=KT[0:64, j, kc * 512:(kc + 1) * 512], in_=bank[j * 64:(j + 1) * 64, :],
                                                          func=AF.Copy),
                            reads=[bkey], writes=["KT"])
                for kc in range(nkc):
                    for t in range(4):
                        add("pe", lambda e: e.matmul(pc[:, t * 128:(t + 1) * 128],
                                                     CKV[:, kc * 512 + t * 128:kc * 512 + (t + 1) * 128],
                                                     wukv[:, 128:256], start=True, stop=True),
                            reads=["wukv", "SH"], writes=["pc"])
                    v_put(pc[:], kc * 4, 4, eng="dve")
                for b in range(NB[sname]):
                    fm(s, 0, 128, sname, b * 512, 512, pa, "pa")
                    fm(s, 128, 128, sname, b * 512, 512, pb, "pb")
                    for i, (bank, bkey) in enumerate(((pa, "pa"), (pb, "pb"))):
                        q = rot("sq", 2)
                        add("act", lambda e: e.activation(out=sqb[q][:], in_=bank[:], func=AF.Square),
                            reads=[bkey], writes=[("sq", q)])
                        add("pe", lambda e: e.matmul(pc[:], ones_bf[:], sqb[q][:], start=(i == 0), stop=(i == 1)),
                            reads=["ones", ("sq", q)], writes=["pc"])
                    add("act", lambda e: e.activation(out=rstd_t[:], in_=pc[:], func=AF.Ln, scale=1.0 / 256, bias=eps_t[:, 0:1]),
                        reads=["pc", "eps"], writes=["of1"])
                    add("act", lambda e: e.activation(out=rstd_t[:], in_=rstd_t[:], func=AF.Exp, scale=-0.5), reads=["of1"], writes=["of1"])
                    for i, (bank, bkey) in enumerate(((pa, "pa"), (pb, "pb"))):
                        add("dve", lambda e: e.scalar_tensor_tensor(out=sqb[i][:], in0=bank[:], scalar=qng_sb[:, l, i:i + 1],
                                                                    in1=rstd_t[:], op0=ALU.mult, op1=ALU.mult),
                            reads=[bkey, "qng", "of1"], writes=[("sq", i)])
                    for j in range(2):
                        for i in range(2):
                            add("pe", lambda e: e.matmul(pa[:], wuq[:, i, j * 256:j * 256 + 128], sqb[i][:],
                                                         start=(i == 0), stop=(i == 1)),
                                reads=["wuq", ("sq", i)], writes=["pa"])
                        if sname == "S":
                            for i in range(2):
                                add("pe", lambda e: e.matmul(pb[:], wuq[:, i, j * 256 + 128:j * 256 + 256], sqb[i][:],
                                                             start=(i == 0), stop=(i == 1)),
                                    reads=["wuq", ("sq", i)], writes=["pb"])
                            rope(pa[0:96, :], "pa", pb[0:96, :], "pb", tabC, slice(0, 96), b, qb[j][0:96, :], ("qb", j), nrow=96, tkey="tabC")
                        else:
                            add("act", lambda e: e.activation(out=qb[j][0:96, :], in_=pa[0:96, :], func=AF.Copy),
                                reads=["pa"], writes=[("qb", j)])
                    z_block(s, 640, sname, b)
                    for j in range(2):
                        oi = rot("o", 2)
                        OT = ops_[oi]
                        attend_set(sname, lambda kt, j=j: KT[:, j, kt * 128:(kt + 1) * 128],
                                   lambda q0, n, j=j: qb[j][:, q0:q0 + n], j, sc, OT, ("o", oi), ("qb", j))
                        normalize_head(OT, ("o", oi), j, of1, "of1", on_act=(j == 1))
                    add("dve", lambda e: e.tensor_tensor(out=yTb[:], in0=of1[:], in1=zs[:], op=ALU.mult), reads=["of1", "zs"], writes=["yTb"])
                    outproj(s, l, sname, b)

        def phase_D(idx):
            kind, l, hp = items[idx]
            s = idx % 2
            sc = 0.125
            if hp == 0:
                add("pool", lambda e: e.dma_start(out=SH[:, 0:4096], in_=ebias_d[l]), writes=["SH"], dma="eb")
                add("dve", lambda e: e.tensor_scalar_mul(out=SH[:, 0:4096], in0=SH[:, 0:4096], scalar1=8.0),
                    reads=["SH"], writes=["SH"])
            for sname in ("P", "S"):
                L = LCTX if sname == "S" else 0
                if sname == "S":
                    add("pool", lambda e: e.dma_start(out=KT[:, 0, 0:LCTX], in_=cdkT_d[l, hp]), writes=["KT"], dma="kv")
                    v_cache(cdv_d, l, hp)
                for b in range(NB[sname]):
                    (A_, Ak), (B_, Bk), (C_, Ck) = banksets[b % 2]
                    fm(s, 128, 128, sname, b * 512, 512, A_, Ak)
                    add("act", lambda e: e.activation(out=KT[:, 0, L + b * 512:L + (b + 1) * 512], in_=A_[:], func=AF.Copy),
                        reads=[Ak], writes=["KT"])
                    if sname == "P":
                        q = rot("stage", 2)
                        add("dve", lambda e: e.tensor_copy(out=stage[q][:], in_=A_[:]), reads=[Ak], writes=[("stage", q)])
                        store(sdkT_d[l, hp * 128:(hp + 1) * 128, :], stage[q][:], ("stage", q))
                    v_block(s, 256, sname, b, l, hp, state_d=(sdv_d if sname == "P" else None), bank=(C_, Ck))
                for b in range(NB[sname]):
                    fm(s, 0, 128, sname, b * 512, 512, pa, "pa")
                    for hh in range(2):
                        add("dve", lambda e: e.tensor_scalar_mul(out=qb[hh][:], in0=pa[:], scalar1=tabA[:, 198 + hh:199 + hh]),
                            reads=["pa", "tabA"], writes=[("qb", hh)])
                    z_block(s, 384, sname, b)
                    for hh in range(2):
                        oi = rot("o", 2)
                        OT = ops_[oi]
                        okey = ("o", oi)
                        hs = slice(hh * 64, (hh + 1) * 64)
                        hglob = 2 * hp + hh
                        klhs = lambda kt: KT[:, 0, kt * 128:(kt + 1) * 128]
                        qh = qb[hh]
                        qkey = ("qb", hh)
                        if sname == "P":
                            attend_set("P", klhs, lambda q0, n, qh=qh: qh[:, q0:q0 + n], hh, sc, OT, okey, qkey)
                        else:
                            attend(klhs, qh[:, :], 512, [0, 1, 2, 3], lambda kt, hh=hh: VA[:, kt, hh, :], sc, OT, okey, 0,
                                   ["KT"], qkey, first=True, last=False)
                            rows = []
                            for r8 in range(8):
                                brow = b * 8 + r8
                                a = min(max(brow - 4, 0), 24)
                                a0 = a - (a % 2)
                                nt = 4 if a % 2 == 0 else 5
                                rows.append((r8, brow, a, a0, nt, rot("s", 3)))

                            def local_s(r8, brow, a, a0, nt, si):
                                S = sps[si]
                                for t in range(nt):
                                    kc0 = LCTX + (a0 + 2 * t) * 64
                                    j0 = 7 + brow - a0 - 2 * t
                                    add("pe", lambda e: e.matmul(
                                        S[:, t * 64:(t + 1) * 64], KT[:, 0, kc0:kc0 + 128], qh[:, r8 * 64:(r8 + 1) * 64],
                                        start=True, stop=False),
                                        reads=["KT", qkey], writes=[("s", si)])
                                    hm = None
                                    if a % 2 == 1 and t == 0:
                                        hm = 0
                                    elif a % 2 == 1 and t == nt - 1:
                                        hm = 1
                                    add("pe", lambda e: e.matmul(
                                        S[:, t * 64:(t + 1) * 64], ident[:], E8[:, hglob, j0, :], start=False, stop=(hm is None)),
                                        reads=["ident", "SH"], writes=[("s", si)])
                                    if hm is not None:
                                        add("pe", lambda e: e.matmul(
                                            S[:, t * 64:(t + 1) * 64], ident[:], HM[:, hm, :], start=False, stop=True),
                                            reads=["ident", "HM"], writes=[("s", si)])

                            local_s(*rows[0])
                            local_s(*rows[1])
                            for (r8, brow, a, a0, nt, si) in rows:
                                S = sps[si]
                                pli = rot("ptl", 2)
                                ptl = PTL[pli]
                                add("act", lambda e: e.activation(
                                    out=ptl[:, 0:nt, :], in_=S[:, 0:nt * 64].rearrange("p (t c) -> p t c", c=64),
                                    func=AF.Exp, scale=sc),
                                    reads=[("s", si)], writes=[("PTL", pli)])
                                if r8 + 2 < 8:
                                    local_s(*rows[r8 + 2])
                                for t in range(nt):
                                    ktile = 4 + (a0 + 2 * t) // 2
                                    ps_ = slice(0, 128)
                                    add("pe", lambda e: e.matmul(
                                        OT[:, r8 * 64:(r8 + 1) * 64], VA[ps_, ktile, hh, :], ptl[ps_, t, :],
                                        start=False, stop=(r8 == 7 and t == nt - 1)),
                                        reads=[("PTL", pli), "VA"], writes=[okey])
                        normalize_head(OT, okey, hh, of1, "of1", on_act=(hh == 1))
                    add("dve", lambda e: e.tensor_tensor(out=yTb[:], in0=of1[:], in1=zs[:], op=ALU.mult), reads=["of1", "zs"], writes=["yTb"])
                    outproj(s, l, sname, b)

        def phase_B(idx):
            kind, l, half = items[idx]
            s = idx % 2
            w0 = convw_sb[:, l, half, 0:1]
            w1 = convw_sb[:, l, half, 1:2]
            w2 = convw_sb[:, l, half, 2:3]
            for sname in ("P", "S"):
                def gview(b, off, sname=sname):
                    if sname == "S":
                        return G_S[:, 1 + b * 512 + off:1 + b * 512 + off + 512]
                    return G_P[:, :, 1 + off:1 + off + 256]
                gkey = "KT" if sname == "S" else "rcp"
                if sname == "S":
                    add("dve", lambda e: e.memset(G_S[:, 0:1], 0.0), writes=[gkey])
                    add("dve", lambda e: e.memset(G_S[:, TS + 1:TS + 2], 0.0), writes=[gkey])
                else:
                    add("dve", lambda e: e.memset(G_P[:, :, 0:1], 0.0), writes=[gkey])
                    add("dve", lambda e: e.memset(G_P[:, :, 257:258], 0.0), writes=[gkey])
                bsets = [[(pa, "pa"), (pb, "pb"), (pc, "pc"), (sps[0], ("s", 0))],
                         [(sps[1], ("s", 1)), (sps[2], ("s", 2)), (ops_[0], ("o", 0)), (ops_[1], ("o", 1))]]
                for b in range(NB[sname]):
                    tk = slice(b * 512, (b + 1) * 512)
                    (Pbb, Kbb), (Pbc, Kbc), (Pbh, Kbh), (Pbz, Kbz) = bsets[b % 2]
                    fm(s, 0, 128, sname, b * 512, 512, Pbb, Kbb)
                    fm(s, 128, 128, sname, b * 512, 512, Pbc, Kbc)
                    fm(s, 256, 128, sname, b * 512, 512, Pbh, Kbh)
                    fm(s, 384, 128, sname, b * 512, 512, Pbz, Kbz)
                    add("act", lambda e: e.activation(out=of1[:], in_=Pbc[:], func=AF.Copy), reads=[Kbc], writes=["of1"])
                    gout = gview(b, 0)
                    if sname == "S":
                        add("dve", lambda e: e.tensor_tensor(out=gout, in0=Pbh[:], in1=of1[:], op=ALU.mult),
                            reads=[Kbh, "of1"], writes=[gkey])
                    else:
                        add("dve", lambda e: e.tensor_tensor(out=gout, in0=Pbh[:].rearrange("p (s t) -> p s t", s=2),
                                                             in1=of1[:].rearrange("p (s t) -> p s t", s=2), op=ALU.mult),
                            reads=[Kbh, "of1"], writes=[gkey])
                    silu_from_ps(Pbz[:], Kbz, of1[:], "of1", 512)
                    add("dve", lambda e: e.tensor_tensor(out=BBZ[:, tk], in0=Pbb[:], in1=of1[:], op=ALU.mult),
                        reads=[Kbb, "of1"], writes=["SH"])
                for b in range(NB[sname]):
                    tk = slice(b * 512, (b + 1) * 512)
                    if sname == "S":
                        cv = of1[:]
                    else:
                        cv = of1[:].rearrange("p (s t) -> p s t", s=2)
                    add("dve", (lambda b=b, cv=cv: lambda e: e.tensor_scalar_mul(out=cv, in0=gview(b, 0), scalar1=w1))(),
                        reads=[gkey, "convw"], writes=["of1"])
                    add("dve", (lambda b=b, cv=cv: lambda e: e.scalar_tensor_tensor(out=cv, in0=gview(b, -1), scalar=w0, in1=cv,
                                                                                     op0=ALU.mult, op1=ALU.add))(),
                        reads=[gkey, "convw", "of1"], writes=["of1"])
                    add("dve", (lambda b=b, cv=cv: lambda e: e.scalar_tensor_tensor(out=cv, in0=gview(b, 1), scalar=w2, in1=cv,
                                                                                     op0=ALU.mult, op1=ALU.add))(),
                        reads=[gkey, "convw", "of1"], writes=["of1"])
                    add("dve", (lambda tk=tk: lambda e: e.tensor_tensor(out=yTb[:], in0=of1[:], in1=BBZ[:, tk], op=ALU.mult))(),
                        reads=["of1", "SH"], writes=["yTb"])
                    outproj(s, l, sname, b)

        PH = {"A": phase_A, "B": phase_B, "C": phase_C, "D": phase_D}

        load_item(0)
        for idx, (kind, l, j) in enumerate(items):
            if idx + 1 < len(items):
                load_item(idx + 1)
            if kind == "ada":
                ada_piece(idx)
            else:
                if kind == "A" and j == 0:
                    layer_prep(l)
                    norm_block(l, "P", 0)
                    for b in range(4):
                        norm_block(l, "S", b)
                PH[kind](idx)
        norm_block(0, "P", 0, final=True)
        for b in range(4):
            norm_block(0, "S", b, final=True)
        P.emit(st)
    return nc


_OFF = {"aq": 0, "ak": 256, "av": 512, "az": 768, "bb": 1024, "bc": 1280, "bh": 1536, "bz": 1792,
        "cq": 2048, "ckv": 2304, "kpe": 2432, "cz": 2464, "dq": 2720, "dk": 2976, "dv": 3232, "dz": 3488}


def _perm32():
    perm = np.zeros(32, np.int64)
    sign = np.zeros(32, np.float32)
    for a in range(2):
        for s in range(2):
            for i in range(8):
                perm[a * 16 + s * 8 + i] = a * 16 + (1 - s) * 8 + i
                sign[a * 16 + s * 8 + i] = -1.0 if s == 0 else 1.0
    return perm, sign


def _rope_rows():
    perm, sign = _perm32()
    half = 16
    inv = (1.0 / (np.float32(10000.0) ** (np.arange(0, half, 2, dtype=np.float32) / np.float32(half)))).astype(np.float32)
    rows = np.arange(32, dtype=np.float32)
    cols = np.arange(64, dtype=np.float32)
    CR = np.ones((32, 32), np.float32)
    SR = np.ones((32, 32), np.float32)
    CC = np.ones((32, 64), np.float32)
    SC = np.ones((32, 64), np.float32)
    for a in range(2):
        for s in range(2):
            for i in range(8):
                f = a * 16 + s * 8 + i
                if a == 0:
                    ang = (rows * inv[i]).astype(np.float32)
                    CR[f] = np.cos(ang)
                    SR[f] = sign[f] * np.sin(ang)
                else:
                    ang = (cols * inv[i]).astype(np.float32)
                    CC[f] = np.cos(ang)
                    SC[f] = sign[f] * np.sin(ang)
    return CR, SR, CC, SC


def _prep_shared(inp):
    perm, sign = _perm32()
    w_in = np.asarray(inp["w_in"], np.float32)
    sh = {}
    ada_w = np.asarray(inp["ada_w"], np.float32)
    sh["w_ada"] = np.ascontiguousarray(ada_w.reshape(DEPTH, 1024, 4, 768).transpose(0, 2, 1, 3))
    sh["adabT"] = np.ascontiguousarray(np.asarray(inp["ada_b"], np.float32).reshape(DEPTH, 24, 128).transpose(2, 0, 1).reshape(128, DEPTH * 24))
    sh["normgT"] = np.ascontiguousarray(np.asarray(inp["norm_g"], np.float32).reshape(DEPTH, 8, 128).transpose(2, 0, 1).reshape(128, DEPTH * 8))
    sh["fnormgT"] = np.ascontiguousarray(np.asarray(inp["final_norm_g"], np.float32).reshape(8, 128).T)
    wA = np.zeros((DEPTH, 2, 1024, 768), np.float32)
    wB = np.zeros((DEPTH, 2, 1024, 512), np.float32)
    wC = np.zeros((DEPTH, 2, 1024, 768), np.float32)
    wD = np.zeros((DEPTH, 2, 1024, 512), np.float32)
    p128 = np.concatenate([perm + 32 * i for i in range(4)])
    for hp in range(2):
        base = hp * 128
        qc = _OFF["aq"] + base + np.arange(128)
        kc = _OFF["ak"] + base + np.arange(128)
        wA[:, hp, :, 0:128] = w_in[:, :, qc]
        wA[:, hp, :, 128:256] = w_in[:, :, _OFF["aq"] + base + p128]
        wA[:, hp, :, 256:384] = w_in[:, :, kc]
        wA[:, hp, :, 384:512] = w_in[:, :, _OFF["ak"] + base + p128]
        wA[:, hp, :, 512:640] = w_in[:, :, _OFF["av"] + base:_OFF["av"] + base + 128]
        wA[:, hp, :, 640:768] = w_in[:, :, _OFF["az"] + base:_OFF["az"] + base + 128]
        for i, nm in enumerate(("bb", "bc", "bh", "bz")):
            wB[:, hp, :, i * 128:(i + 1) * 128] = w_in[:, :, _OFF[nm] + base:_OFF[nm] + base + 128]
        wC[:, hp, :, 0:256] = w_in[:, :, _OFF["cq"]:_OFF["cq"] + 256]
        wC[:, hp, :, 256:384] = w_in[:, :, _OFF["ckv"]:_OFF["ckv"] + 128]
        kpe_cols = _OFF["kpe"] + np.arange(32)
        for q4 in range(4):
            wC[:, hp, :, 384 + q4 * 32:384 + (q4 + 1) * 32] = w_in[:, :, kpe_cols]
            wC[:, hp, :, 512 + q4 * 32:512 + (q4 + 1) * 32] = w_in[:, :, kpe_cols]
        wC[:, hp, :, 512 + 64:512 + 96] = w_in[:, :, _OFF["kpe"] + perm]
        wC[:, hp, :, 640:768] = w_in[:, :, _OFF["cz"] + base:_OFF["cz"] + base + 128]
        for i, nm in enumerate(("dq", "dk", "dv", "dz")):
            wD[:, hp, :, i * 128:(i + 1) * 128] = w_in[:, :, _OFF[nm] + base:_OFF[nm] + base + 128]
    sh["w_A"], sh["w_B"], sh["w_C"], sh["w_D"] = wA, wB, wC, wD
    sh["w_out"] = np.ascontiguousarray(np.asarray(inp["w_out"], np.float32))
    wuq_in = np.asarray(inp["mla_w_uq"], np.float32)
    wuq = np.zeros((DEPTH, 2, 256, 512), np.float32)
    for hp in range(2):
        for j in range(2):
            h = 2 * hp + j
            wuq[:, hp, :, j * 256:j * 256 + 96] = wuq_in[:, :, h * 96:(h + 1) * 96]
            wuq[:, hp, :, j * 256 + 128:j * 256 + 192] = wuq_in[:, :, h * 96:h * 96 + 64]
            wuq[:, hp, :, j * 256 + 192:j * 256 + 224] = wuq_in[:, :, h * 96 + 64 + perm]
    sh["w_uq"] = wuq
    wukv_in = np.asarray(inp["mla_w_ukv"], np.float32)
    wukv = np.zeros((DEPTH, 2, 128, 256), np.float32)
    for hp in range(2):
        for j in range(2):
            h = 2 * hp + j
            wukv[:, hp, :, j * 64:(j + 1) * 64] = wukv_in[:, :, h * 128:h * 128 + 64]
            wukv[:, hp, :, 128 + j * 64:128 + (j + 1) * 64] = wukv_in[:, :, h * 128 + 64:h * 128 + 128]
    sh["w_ukv"] = wukv
    sh["qng"] = np.ascontiguousarray(np.asarray(inp["mla_q_norm_g"], np.float32).reshape(DEPTH, 2, 128).transpose(2, 0, 1).reshape(128, DEPTH * 2))
    sh["kvng"] = np.ascontiguousarray(np.asarray(inp["mla_kv_norm_g"], np.float32).T)
    sh["lam"] = np.ascontiguousarray(np.asarray(inp["da_lambda"], np.float32).reshape(DEPTH, 128))
    sh["subgc"] = np.ascontiguousarray(np.tile(np.asarray(inp["da_subln_g"], np.float32), (1, 2)).T)
    cw = np.asarray(inp["conv_w"], np.float32)
    sh["convw"] = np.ascontiguousarray(cw.reshape(DEPTH, 3, 2, 128).transpose(3, 0, 2, 1).reshape(128, DEPTH * 6))
    rpb = np.asarray(inp["na_rpb"], np.float32)
    kc = np.arange(64)[:, None]
    qc = np.arange(64)[None, :]
    ws = np.clip(qc - 8, 0, 48)
    inwin = (kc >= ws) & (kc < ws + 16)
    dx = np.clip(kc - qc + 15, 0, 30)
    eb = np.full((DEPTH, 2, 64, 4, 16, 64), -30000.0, np.float32)
    for half in range(2):
        for j in range(16):
            dy = 14 - j + half
            if 0 <= dy <= 14:
                g = rpb[:, :, dy, :][:, :, dx]
                g = np.where(inwin[None, None], g, np.float32(-30000.0))
                eb[:, half, :, :, j, :] = g.transpose(0, 2, 1, 3)
    sh["ebias"] = np.ascontiguousarray(eb.reshape(DEPTH, 128, 4 * 16 * 64))
    CR, SR, CC, SC = _rope_rows()
    tabA = np.zeros((128, 200), np.float32)
    for hh in range(2):
        for c in range(2):
            r = slice(hh * 64 + c * 32, hh * 64 + c * 32 + 32)
            tabA[r, 0:32] = CR
            tabA[r, 32:64] = SR
            tabA[r, 64:128] = CC
            tabA[r, 128:192] = SC
            tabA[r, 192 + c] = 1.0
            tabA[r, 194 + hh * 2 + c] = 1.0
            tabA[r, 198 + hh] = 1.0
    sh["tabA"] = tabA
    tabC = np.zeros((96, 192), np.float32)
    tabC[0:64, 0:32] = 1.0
    tabC[0:64, 32:64] = 0.0
    tabC[0:64, 64:128] = 1.0
    tabC[0:64, 128:192] = 1.0
    tabC[64:96, 0:32] = CR
    tabC[64:96, 32:64] = SR
    tabC[64:96, 64:128] = CC
    tabC[64:96, 128:192] = SC
    sh["tabC"] = tabC
    return sh


_NC_CACHE = {}


def kernel(**inp):
    sh = _prep_shared(inp)
    x_prompt = np.asarray(inp["x_prompt"], np.float32)
    x_sample = np.asarray(inp["x_sample"], np.float32)
    c = np.asarray(inp["c"], np.float32)
    c_ctx = np.asarray(inp["c_ctx"], np.float32)
    cak = np.asarray(inp["cache_a_k"], np.float32)
    cav = np.asarray(inp["cache_a_v"], np.float32)
    cckv = np.asarray(inp["cache_c_kv"], np.float32)
    ckpe = np.asarray(inp["cache_c_kpe"], np.float32)
    cdk = np.asarray(inp["cache_d_k"], np.float32)
    cdv = np.asarray(inp["cache_d_v"], np.float32)
    ncores_run = NCORES
    in_maps = []
    for core in range(ncores_run):
        b = core // 2
        m = dict(sh)
        m["xpT"] = np.ascontiguousarray(x_prompt[2 * core:2 * core + 2].reshape(TP, 1024).T)
        m["xsT"] = np.ascontiguousarray(x_sample[b].T)
        m["condT"] = np.ascontiguousarray(np.stack([c_ctx, c[b]], axis=1))
        m["cakT"] = np.ascontiguousarray(cak[b].reshape(DEPTH, LCTX, 2, 128).transpose(0, 2, 3, 1))
        m["cav"] = np.ascontiguousarray(cav[b].reshape(DEPTH, LCTX, 256))
        m["cckvT"] = np.ascontiguousarray(cckv[b].transpose(0, 2, 1))
        m["ckpeT"] = np.ascontiguousarray(ckpe[b].transpose(0, 2, 1))
        m["cdkT"] = np.ascontiguousarray(cdk[b].reshape(DEPTH, LCTX, 2, 128).transpose(0, 2, 3, 1))
        m["cdv"] = np.ascontiguousarray(cdv[b].reshape(DEPTH, LCTX, 256))
        in_maps.append(m)
    if "nc" not in _NC_CACHE:
        _NC_CACHE["nc"] = build_program()
    res = run_bass_kernel_spmd(_NC_CACHE["nc"], in_maps, core_ids=list(range(ncores_run)))
    R = res.results
    y_prompt = np.zeros((16, 256, 1024), np.float32)
    y_sample = np.zeros((4, TS, 1024), np.float32)
    s_ak = np.zeros((16, DEPTH, 256, 4, 64), np.float32)
    s_av = np.zeros((16, DEPTH, 256, 4, 64), np.float32)
    s_ckv = np.zeros((16, DEPTH, 256, 128), np.float32)
    s_kpe = np.zeros((16, DEPTH, 256, 32), np.float32)
    s_dk = np.zeros((16, DEPTH, 256, 4, 64), np.float32)
    s_dv = np.zeros((16, DEPTH, 256, 4, 64), np.float32)
    for core in range(ncores_run):
        r = R[core]
        y_prompt[2 * core:2 * core + 2] = np.asarray(r["ypT"]).T.reshape(2, 256, 1024)
        if core % 2 == 0:
            y_sample[core // 2] = np.asarray(r["ysT"]).T
        for (dst, key, tmaj) in ((s_ak, "sakT", False), (s_av, "sav", True), (s_dk, "sdkT", False), (s_dv, "sdv", True)):
            a = np.asarray(r[key])
            if not tmaj:
                a = a.transpose(0, 2, 1)
            a = a.reshape(DEPTH, 2, 256, 4, 64).transpose(1, 0, 2, 3, 4)
            dst[2 * core:2 * core + 2] = a
        a = np.asarray(r["sckvT"]).transpose(0, 2, 1).reshape(DEPTH, 2, 256, 128).transpose(1, 0, 2, 3)
        s_ckv[2 * core:2 * core + 2] = a
        a = np.asarray(r["skpeT"]).transpose(0, 2, 1).reshape(DEPTH, 2, 256, 32).transpose(1, 0, 2, 3)
        s_kpe[2 * core:2 * core + 2] = a
    return (y_prompt, y_sample, s_ak, s_av, s_ckv, s_kpe, s_dk, s_dv)
```

```python
import math
from contextlib import ExitStack
import numpy as np
import concourse.bass as bass
import concourse.mybir as mybir
from concourse.bass_utils import run_bass_kernel_spmd

F32 = mybir.dt.float32
BF16 = mybir.dt.bfloat16
ALU = mybir.AluOpType
AF = mybir.ActivationFunctionType
AX = mybir.AxisListType

DEPTH = 4
EPS = 1e-6
NCORES = 8
TS = 2048
TP = 512
LCTX = 512


class _Rec:
    def __init__(self):
        self.call = None

    def __getattr__(self, name):
        def f(*a, **k):
            self.call = (name, a, k)
            return self
        return f


class Prog:
    ENGS = ("pe", "act", "dve", "pool", "sp")

    def __init__(self, nc):
        self.nc = nc
        self.ops = []
        self.reg = {}
        self.dma_cum = {}

    def add(self, eng, fn, reads=(), writes=(), dma=None, late_keys=None):
        writes = list(writes)
        for k in reads:
            if (k in ("pa", "pb", "pc") or (isinstance(k, tuple) and k[0] in ("s", "o"))) and k not in writes:
                writes.append(k)
        rec = _Rec()
        fn(rec)
        name_, a_, k_ = rec.call
        fn = (lambda name_=name_, a_=a_, k_=k_: lambda e: getattr(e, name_)(*a_, **k_))()
        i = len(self.ops)
        deps = {}
        strict = set()

        def note(d, k):
            if eng == "pe" and (late_keys is None or k not in late_keys):
                strict.add(d)

        for k in reads:
            e = self.reg.get(k)
            if e is not None and e[0] is not None:
                deps[e[0]] = "raw"
                note(e[0], k)
        for k in writes:
            e = self.reg.get(k)
            if e is not None:
                if e[0] is not None:
                    if e[0] not in deps:
                        if not (dma is not None and self.ops[e[0]]["dma"] is not None):
                            deps[e[0]] = "waw"
                    note(e[0], k)
                for r in e[1]:
                    if r not in deps:
                        deps[r] = "war"
                    note(r, k)
        deps.pop(i, None)
        for k in reads:
            self.reg.setdefault(k, [None, []])[1].append(i)
        for k in writes:
            self.reg[k] = [i, []]
        op = dict(eng=eng, fn=fn, deps=deps, dma=dma, cum=None, snap=None, strict=strict)
        if dma is not None:
            self.dma_cum[dma] = self.dma_cum.get(dma, 0) + 16
            op["cum"] = self.dma_cum[dma]
        op["snap"] = dict(self.dma_cum)
        self.ops.append(op)
        return i

    def emit(self, stack):
        nc = self.nc
        ops = self.ops

        def skip(op, dop, kind):
            if dop["dma"] is None and dop["eng"] == op["eng"] and op["dma"] is None:
                if op["eng"] == "pe":
                    return True
            return False

        need = set()
        for i, op in enumerate(ops):
            for d, kind in op["deps"].items():
                dop = ops[d]
                if dop["dma"] is not None or skip(op, dop, kind):
                    continue
                need.add(d)
        sig = {}
        cnt = {e: 0 for e in self.ENGS}
        for i, op in enumerate(ops):
            if op["dma"] is None and i in need:
                cnt[op["eng"]] += 1
                sig[i] = cnt[op["eng"]]
        sems = {e: stack.enter_context(nc.semaphore("s_" + e)) for e in self.ENGS}
        dsems = {n: stack.enter_context(nc.semaphore("d_" + n)) for n in self.dma_cum}
        per_eng = {e: [] for e in self.ENGS}
        seen = {e: {} for e in self.ENGS}
        for i, op in enumerate(ops):
            waits = {}
            for d, kind in op["deps"].items():
                dop = ops[d]
                if dop["dma"] is not None:
                    key = ("d", dop["dma"])
                    val = op["snap"][dop["dma"]]
                else:
                    if skip(op, dop, kind):
                        continue
                    key = ("e", dop["eng"])
                    val = sig[d]
                st_ = d in op["strict"]
                if key not in waits:
                    waits[key] = [val, st_]
                else:
                    waits[key][0] = max(waits[key][0], val)
                    waits[key][1] = waits[key][1] or st_
            wl = []
            sn = seen[op["eng"]]
            for key, (val, st_) in waits.items():
                if sn.get(key, 0) >= val:
                    continue
                sn[key] = val
                wl.append((sems[key[1]] if key[0] == "e" else dsems[key[1]], val, st_))
            per_eng[op["eng"]].append((i, op, wl))
        final_waits = [(dsems[n], v) for n, v in self.dma_cum.items()]
        self.n_attached = sum(1 for e_ in per_eng.values() for (_, o_, w_) in e_ if o_["dma"] is None and any(not x[2] for x in w_))
        final_eng = [(sems[e], cnt[e]) for e in self.ENGS if cnt[e] > 0]

        def run(eng_name, e):
            for i, op, wl in per_eng[eng_name]:
                attach = None
                if op["dma"] is None:
                    cands = [w for w in wl if not w[2]]
                    if cands:
                        attach = cands[-1]
                for w in wl:
                    if w is not attach:
                        e.wait_ge(w[0], w[1])
                ins = op["fn"](e)
                if attach is not None:
                    ins._wait_ge(attach[0], attach[1])
                if op["dma"] is not None:
                    ins.then_inc(dsems[op["dma"]], 16)
                elif i in sig:
                    ins.then_inc(sems[eng_name], 1)
            if eng_name == "sp":
                for s, v in final_waits:
                    e.wait_ge(s, v)
                for s, v in final_eng:
                    e.wait_ge(s, v)

        with nc.Block() as block:
            @block.tensor
            def _(e):
                run("pe", e)

            @block.scalar
            def _(e):
                run("act", e)

            @block.vector
            def _(e):
                run("dve", e)

            @block.gpsimd
            def _(e):
                run("pool", e)

            @block.sync
            def _(e):
                run("sp", e)


def lam_init_of(l):
    return 0.8 - 0.6 * math.exp(-0.3 * l)


def build_program():
    nc = bass.Bass("TRN2", target_bir_lowering=False)

    def din(name, shape):
        return nc.dram_tensor(name, list(shape), F32, kind="ExternalInput").ap()

    def dout(name, shape):
        return nc.dram_tensor(name, list(shape), F32, kind="ExternalOutput").ap()

    xpT_d = din("xpT", [1024, TP])
    xsT_d = din("xsT", [1024, TS])
    condT_d = din("condT", [1024, 2])
    w_ada_d = din("w_ada", [DEPTH, 4, 1024, 768])
    adabT_d = din("adabT", [128, DEPTH * 24])
    normgT_d = din("normgT", [128, DEPTH * 8])
    fnormgT_d = din("fnormgT", [128, 8])
    w_A_d = din("w_A", [DEPTH, 2, 1024, 768])
    w_B_d = din("w_B", [DEPTH, 2, 1024, 512])
    w_C_d = din("w_C", [DEPTH, 2, 1024, 768])
    w_D_d = din("w_D", [DEPTH, 2, 1024, 512])
    w_out_d = din("w_out", [DEPTH, 1024, 1024])
    w_uq_d = din("w_uq", [DEPTH, 2, 256, 512])
    w_ukv_d = din("w_ukv", [DEPTH, 2, 128, 256])
    qng_d = din("qng", [128, DEPTH * 2])
    kvng_d = din("kvng", [128, DEPTH])
    lam_d = din("lam", [DEPTH, 128])
    subgc_d = din("subgc", [128, DEPTH])
    convw_d = din("convw", [128, DEPTH * 2 * 3])
    ebias_d = din("ebias", [DEPTH, 128, 4 * 16 * 64])
    cakT_d = din("cakT", [DEPTH, 2, 128, LCTX])
    cav_d = din("cav", [DEPTH, LCTX, 256])
    cckvT_d = din("cckvT", [DEPTH, 128, LCTX])
    ckpeT_d = din("ckpeT", [DEPTH, 32, LCTX])
    cdkT_d = din("cdkT", [DEPTH, 2, 128, LCTX])
    cdv_d = din("cdv", [DEPTH, LCTX, 256])
    tabA_d = din("tabA", [128, 200])
    tabC_d = din("tabC", [96, 192])

    ypT_d = dout("ypT", [1024, TP])
    ysT_d = dout("ysT", [1024, TS])
    sakT_d = dout("sakT", [DEPTH, 256, TP])
    sav_d = dout("sav", [DEPTH, TP, 256])
    sckvT_d = dout("sckvT", [DEPTH, 128, TP])
    skpeT_d = dout("skpeT", [DEPTH, 32, TP])
    sdkT_d = dout("sdkT", [DEPTH, 256, TP])
    sdv_d = dout("sdv", [DEPTH, TP, 256])

    with ExitStack() as st:
        def sb(name, shape, dt):
            return st.enter_context(nc.sbuf_tensor(name, list(shape), dt))

        def psum(name, shape, dt):
            return st.enter_context(nc.psum_tensor(name, list(shape), dt))

        P = Prog(nc)
        add = P.add

        xT = {"S": sb("xTs", [128, 8, TS], F32), "P": sb("xTp", [128, 8, TP], F32)}
        hT = {"S": sb("hTs", [128, 8, TS], BF16), "P": sb("hTp", [128, 8, TP], BF16)}
        NT = {"S": TS, "P": TP}
        NB = {"S": 4, "P": 1}
        wslot = [sb("wslot0", [128, 8, 768], BF16), sb("wslot1", [128, 8, 768], BF16)]
        woslot = [sb("wo0", [128, 1024], BF16), sb("wo1", [128, 1024], BF16)]
        KT = sb("KT", [128, 2, LCTX + TS], BF16)
        VA = sb("VA", [128, 20, 2, 128], BF16)
        SH = sb("SH", [128, 4096], BF16)
        CKV = SH[:, 0:LCTX + TS]
        qb = [sb("qb%d" % i, [128, 512], BF16) for i in range(2)]
        HM = sb("HM", [128, 2, 64], BF16)
        PT = [sb("PT%d" % i, [128, 512], BF16) for i in range(3)]
        PTL = [sb("PTL%d" % i, [128, 5, 64], BF16) for i in range(2)]
        zs = sb("zs", [128, 512], F32)
        t2b = sb("t2b", [128, 512], F32)
        rcp = sb("rcp", [128, 516], F32)
        blockones = sb("blockones", [128, 128], BF16)
        yTb = sb("yTb", [128, 512], BF16)
        sqb = [sb("sq0", [128, 512], BF16), sb("sq1", [128, 512], BF16)]
        qb = qb + sqb
        qkeys = [("qb", 0), ("qb", 1), ("sq", 0), ("sq", 1)]
        stage = [sb("stage0", [128, 512], F32), sb("stage1", [128, 512], F32)]
        tmpf = stage
        of1 = sb("of1", [128, 512], F32)
        rstd_t = of1
        of2 = sb("of2", [128, 512], F32)
        small = sb("small", [128, 64], F32)
        mods = [sb("mod0", [128, 24, 2], F32), sb("mod1", [128, 24, 2], F32)]
        gmod = sb("gmod", [128, 8, 2], F32)
        cond_sb = sb("cond_sb", [128, 8, 2], F32)
        csil = sb("csil", [128, 8, 2], BF16)
        adab_sb = sb("adab_sb", [128, DEPTH, 24], F32)
        normg_sb = sb("normg_sb", [128, DEPTH, 8], F32)
        fnormg_sb = sb("fnormg_sb", [128, 8], F32)
        qng_sb = sb("qng_sb", [128, DEPTH, 2], F32)
        kvng_sb = sb("kvng_sb", [128, DEPTH], F32)
        lam_sb = sb("lam_sb", [128, 128], F32)
        subgc_sb = sb("subgc_sb", [128, DEPTH], F32)
        subgs = sb("subgs", [128, 1], F32)
        lamv = sb("lamv", [128, 8], F32)
        convw_sb = sb("convw_sb", [128, DEPTH, 2, 3], F32)
        tabA = sb("tabA_sb", [128, 200], F32)
        tabC = sb("tabC_sb", [96, 192], F32)
        E8 = SH[:, 0:4096].rearrange("p (h j c) -> p h j c", h=4, j=16)
        ones_bf = sb("ones_bf", [128, 128], BF16)
        ident = sb("ident", [128, 128], BF16)
        wuq = sb("wuq", [128, 2, 512], BF16)
        wukv = sb("wukv", [128, 256], BF16)
        G_S = KT[:].rearrange("p a b -> p (a b)").bitcast(F32)[:, 0:TS + 2]
        G_P = rcp[:, 0:516].rearrange("p (s t) -> p s t", s=2)
        BBZ = SH[:, 0:TS]

        pa = psum("pa", [128, 512], F32)
        pb = psum("pb", [128, 512], F32)
        pc = psum("pc", [128, 512], F32)
        sps = [psum("s%d" % i, [128, 512], F32) for i in range(3)]
        ops_ = [psum("oA", [128, 512], F32), psum("oB", [128, 512], F32)]
        pcbf = pc[:].bitcast(BF16)
        proj_banks = [(pa, "pa"), (pb, "pb"), (pc, "pc")]
        pp_banks = [(pa, "pa"), (pb, "pb"), (pc, "pc")]
        op_banks = [(sps[0], ("s", 0)), (sps[1], ("s", 1)), (sps[2], ("s", 2)), (pc, "pc"), (pb, "pb"), (pa, "pa")]
        banksets = [[(pa, "pa"), (pb, "pb"), (pc, "pc")], [(sps[0], ("s", 0)), (sps[1], ("s", 1)), (sps[2], ("s", 2))]]

        cnt = {"opj": 0, "ppj": 0, "s": 0, "pt": 0, "o": 0, "sq": 0, "tmp": 0, "stage": 0, "ptl": 0, "pj": 0}

        def rot(name, n):
            v = cnt[name] % n
            cnt[name] += 1
            return v

        add("pool", lambda e: e.memset(ones_bf[:], 1.0), writes=["ones"])
        add("pool", lambda e: e.memset(ident[:], 0.0), writes=["ident"])
        add("pool", lambda e: e.affine_select(out=ident[:], in_=ident[:], compare_op=ALU.not_equal, fill=1.0,
                                              base=0, pattern=[[-1, 128]], channel_multiplier=1),
            reads=["ident"], writes=["ident"])
        add("pool", lambda e: e.memset(VA[:], 1.0), writes=["VA"])
        add("pool", lambda e: e.memset(HM[:], 0.0), writes=["HM"])
        add("pool", lambda e: e.memset(HM[0:64, 0, :], -240000.0), writes=["HM"])
        add("pool", lambda e: e.memset(HM[64:128, 1, :], -240000.0), writes=["HM"])
        add("pool", lambda e: e.memset(blockones[:], 0.0), writes=["blockones"])
        add("pool", lambda e: e.memset(blockones[0:64, 0:64], 1.0), writes=["blockones"])
        add("pool", lambda e: e.memset(blockones[64:128, 64:128], 1.0), writes=["blockones"])

        def ld(dst, src, key, sem="misc", eng="sp"):
            add(eng, lambda e: e.dma_start(out=dst, in_=src), writes=[key], dma=sem)

        ld(cond_sb[:], condT_d.rearrange("(k p) c -> p k c", p=128), "cond")
        ld(adab_sb[:], adabT_d.rearrange("p (l c) -> p l c", l=DEPTH), "adab")
        ld(normg_sb[:], normgT_d.rearrange("p (l c) -> p l c", l=DEPTH), "normg")
        ld(fnormg_sb[:], fnormgT_d, "fnormg")
        ld(qng_sb[:], qng_d.rearrange("p (l c) -> p l c", l=DEPTH), "qng")
        ld(kvng_sb[:], kvng_d, "kvng")
        ld(convw_sb[:], convw_d.rearrange("p (l h c) -> p l h c", l=DEPTH, h=2), "convw")
        ld(tabA[:], tabA_d, "tabA")
        ld(subgc_sb[:], subgc_d, "subgc")
        ld(tabC[:], tabC_d, "tabC")
        for k in range(8):
            ld(xT["P"][:, k, :], xpT_d[k * 128:(k + 1) * 128, :], ("x", "P", 0), sem="xin")
            for b in range(4):
                ld(xT["S"][:, k, b * 512:(b + 1) * 512], xsT_d[k * 128:(k + 1) * 128, b * 512:(b + 1) * 512],
                   ("x", "S", b), sem="xin")

        add("act", lambda e: e.activation(out=small[:, 0:16], in_=cond_sb[:].rearrange("p k c -> p (k c)"),
                                          func=AF.Exp, scale=-1.0), reads=["cond"], writes=["small"])
        add("dve", lambda e: e.tensor_scalar_add(out=small[:, 0:16], in0=small[:, 0:16], scalar1=1.0),
            reads=["small"], writes=["small"])
        add("dve", lambda e: e.reciprocal(out=small[:, 0:16], in_=small[:, 0:16]), reads=["small"], writes=["small"])
        add("dve", lambda e: e.tensor_tensor(out=csil[:].rearrange("p k c -> p (k c)"), in0=small[:, 0:16],
                                             in1=cond_sb[:].rearrange("p k c -> p (k c)"), op=ALU.mult),
            reads=["small", "cond"], writes=["csil"])

        items = []
        for j in range(4):
            items.append(("ada", 0, j))
        for l in range(DEPTH):
            nxt = [("ada", l + 1, j) for j in range(4)] if l + 1 < DEPTH else []
            order = [("A", l, 0), ("A", l, 1)] + nxt[0:1] + [("B", l, 0), ("B", l, 1)] + nxt[1:2] + \
                    [("C", l, 0), ("C", l, 1)] + nxt[2:3] + [("D", l, 0)] + nxt[3:4] + [("D", l, 1)]
            items.extend(order)
        WIDTH = {"ada": 768, "A": 768, "B": 512, "C": 768, "D": 512}
        WSRC = {"ada": w_ada_d, "A": w_A_d, "B": w_B_d, "C": w_C_d, "D": w_D_d}
        GOFF = {"A": 0, "B": 256, "C": 512, "D": 768}

        def load_item(idx):
            kind, l, j = items[idx]
            s = idx % 2
            wd = WIDTH[kind]
            src = WSRC[kind][l, j]
            for k in range(8):
                add("pool", (lambda k=k: lambda e: e.dma_start(out=wslot[s][:, k, 0:wd],
                                                               in_=src[k * 128:(k + 1) * 128, :]))(),
                    writes=[("w", s, k)], dma="w%d" % s)
            if kind != "ada":
                r0 = GOFF[kind] + j * 128
                add("pool", lambda e: e.dma_start(out=woslot[s][:], in_=w_out_d[l, r0:r0 + 128, :]),
                    writes=[("wo", s)], dma="wo%d" % s)

        def wkeys(s):
            return [("w", s, k) for k in range(8)]

        def fm(s, c0, M, sname, tok0, n, ps, pskey, extra_reads=()):
            for k in range(8):
                add("pe", (lambda k=k: lambda e: e.matmul(ps[0:M, 0:n], wslot[s][:, k, c0:c0 + M],
                                                          hT[sname][:, k, tok0:tok0 + n],
                                                          start=(k == 0), stop=(k == 7)))(),
                    reads=[("w", s, k), ("h", sname, tok0 // 512)] + list(extra_reads), writes=[pskey])

        def tm(s, c0, N, sname, tok0, ps_ap, pskey):
            for k in range(8):
                add("pe", (lambda k=k: lambda e: e.matmul(ps_ap, hT[sname][:, k, tok0:tok0 + 128],
                                                          wslot[s][:, k, c0:c0 + N],
                                                          start=(k == 0), stop=(k == 7)))(),
                    reads=[("w", s, k), ("h", sname, tok0 // 512)], writes=[pskey])

        def rstd_from_ps(ps, n_feat, rkey="of1"):
            add("act", lambda e: e.activation(out=rstd_t[:], in_=ps[:], func=AF.Ln, scale=1.0 / n_feat,
                                              bias=eps_t[:, 0:1]),
                reads=[rkey + "_ps", "eps"], writes=["of1"])
            add("act", lambda e: e.activation(out=rstd_t[:], in_=rstd_t[:], func=AF.Exp, scale=-0.5),
                reads=["of1"], writes=["of1"])

        eps_t = sb("eps_t", [128, 1], F32)
        add("pool", lambda e: e.memset(eps_t[:], EPS), writes=["eps"])
        one_t = sb("one_t", [128, 1], F32)
        add("pool", lambda e: e.memset(one_t[:], 1.0), writes=["one"])

        def silu_from_ps(ps_ap, pskey, out_ap, outkey, shape_n):
            sc = of2[:, 0:shape_n]
            add("act", lambda e: e.activation(out=sc, in_=ps_ap, func=AF.Exp, scale=-1.0), reads=[pskey], writes=["of2"])
            add("act", lambda e: e.activation(out=sc, in_=sc, func=AF.Ln, bias=one_t[:, 0:1], scale=1.0), reads=["of2", "one"], writes=["of2"])
            add("act", lambda e: e.activation(out=sc, in_=sc, func=AF.Exp, scale=-1.0), reads=["of2"], writes=["of2"])
            add("dve", lambda e: e.tensor_tensor(out=out_ap, in0=ps_ap, in1=sc, op=ALU.mult),
                reads=[pskey, "of2"], writes=[outkey])

        def store(dst_ap, src_ap, srckey, sem=None):
            sem = "st%d" % srckey[1]
            add("sp", lambda e: e.dma_start(out=dst_ap, in_=src_ap), reads=[srckey], dma=sem)

        def rope(ps_x, xkey, ps_p, pkey, tab, rows, blk, out_ap, outkey, nrow=128, tkey="tabA", scratch=None):
            r0 = blk * 8
            CR = tab[rows, r0:r0 + 8].unsqueeze(2).broadcast_to([nrow, 8, 64])
            SR = tab[rows, 32 + r0:32 + r0 + 8].unsqueeze(2).broadcast_to([nrow, 8, 64])
            CC = tab[rows, 64:128].unsqueeze(1).broadcast_to([nrow, 8, 64])
            SC = tab[rows, 128:192].unsqueeze(1).broadcast_to([nrow, 8, 64])
            v = lambda ap: ap.rearrange("p (r c) -> p r c", c=64)
            (s1_, k1_), (s2_, k2_) = scratch if scratch is not None else ((of1, "of1"), (of2, "of2"))
            t1 = s1_[rows, 0:512]
            t2 = s2_[rows, 0:512]
            add("dve", lambda e: e.tensor_tensor(out=v(t1), in0=v(ps_x), in1=CR, op=ALU.mult), reads=[xkey, tkey], writes=[k1_])
            add("dve", lambda e: e.tensor_tensor(out=v(t1), in0=v(t1), in1=CC, op=ALU.mult), reads=[k1_, tkey], writes=[k1_])
            add("dve", lambda e: e.tensor_tensor(out=v(t2), in0=v(ps_p), in1=SR, op=ALU.mult), reads=[pkey, tkey], writes=[k2_])
            add("dve", lambda e: e.tensor_tensor(out=v(t2), in0=v(t2), in1=SC, op=ALU.mult), reads=[k2_, tkey], writes=[k2_])
            add("dve", lambda e: e.tensor_tensor(out=out_ap, in0=t1, in1=t2, op=ALU.add), reads=[k1_, k2_], writes=[outkey])

        def ada_piece(idx):
            kind, l, j = items[idx]
            s = idx % 2
            for m in range(6):
                for k in range(8):
                    add("pe", (lambda m=m, k=k: lambda e: e.matmul(pa[:, 2 * m:2 * m + 2], wslot[s][:, k, m * 128:(m + 1) * 128],
                                                                   csil[:, k, :], start=(k == 0), stop=(k == 7)))(),
                        reads=[("w", s, k), "csil"], writes=["pa"])
            mod = mods[l % 2]
            add("dve", lambda e: e.tensor_tensor(out=mod[:, 6 * j:6 * j + 6, :],
                                                 in0=pa[:, 0:12].rearrange("p (m c) -> p m c", c=2),
                                                 in1=adab_sb[:, l, 6 * j:6 * j + 6].unsqueeze(2).broadcast_to([128, 6, 2]),
                                                 op=ALU.add),
                reads=["pa", "adab"], writes=[("mod", l % 2, j)])

        def layer_prep(l):
            mod = mods[l % 2]
            add("dve", lambda e: e.tensor_scalar_add(out=gmod[:], in0=mod[:, 8:16, :], scalar1=1.0),
                reads=[("mod", l % 2, jj) for jj in range(4)], writes=["gmod"])
            add("dve", lambda e: e.tensor_tensor(out=gmod[:], in0=gmod[:],
                                                 in1=normg_sb[:, l, :].unsqueeze(2).broadcast_to([128, 8, 2]), op=ALU.mult),
                reads=["gmod", "normg"], writes=["gmod"])
            li = lam_init_of(l)
            ld(lam_sb[:], lam_d[l:l + 1, :].partition_broadcast(128), "lam", sem="lamld")
            add("dve", lambda e: e.tensor_tensor(out=small[:, 0:32], in0=lam_sb[:, 0:32], in1=lam_sb[:, 32:64], op=ALU.mult),
                reads=["lam"], writes=["small"])
            add("dve", lambda e: e.tensor_tensor(out=small[:, 32:64], in0=lam_sb[:, 64:96], in1=lam_sb[:, 96:128], op=ALU.mult),
                reads=["lam", "small"], writes=["small"])
            add("dve", lambda e: e.reduce_sum(out=lamv[:, 0:2], in_=small[:, 0:64].rearrange("p (a b) -> p a b", b=32), axis=AX.X),
                reads=["small"], writes=["lamv"])
            add("act", lambda e: e.activation(out=lamv[:, 2:4], in_=lamv[:, 0:2], func=AF.Exp), reads=["lamv"], writes=["lamv2"])
            add("dve", lambda e: e.tensor_tensor(out=lamv[:, 4:5], in0=lamv[:, 3:4], in1=lamv[:, 2:3], op=ALU.subtract),
                reads=["lamv2"], writes=["lamv3"])
            add("dve", lambda e: e.tensor_scalar_add(out=lamv[:, 5:6], in0=lamv[:, 4:5], scalar1=-li), reads=["lamv3"], writes=["neglam"])
            add("dve", lambda e: e.tensor_scalar_mul(out=subgs[:], in0=subgc_sb[:, l:l + 1], scalar1=(1.0 - li)),
                reads=["subgc"], writes=["subgs"])

        def norm_block(l, sname, b, final=False):
            ci = 1 if sname == "S" else 0
            x = xT[sname]
            tk = slice(b * 512, (b + 1) * 512)
            for k in range(8):
                q = rot("sq", 2)
                add("act", (lambda k=k, q=q: lambda e: e.activation(out=sqb[q][:], in_=x[:, k, tk], func=AF.Square))(),
                    reads=[("x", sname, b)], writes=[("sq", q)])
                add("pe", (lambda k=k, q=q: lambda e: e.matmul(pa[:], ones_bf[:], sqb[q][:], start=(k == 0), stop=(k == 7)))(),
                    reads=["ones", ("sq", q)], writes=["pa"])
            add("act", lambda e: e.activation(out=rstd_t[:], in_=pa[:], func=AF.Ln, scale=1.0 / 1024, bias=eps_t[:, 0:1]),
                reads=["pa", "eps"], writes=["of1"])
            add("act", lambda e: e.activation(out=rstd_t[:], in_=rstd_t[:], func=AF.Exp, scale=-0.5), reads=["of1"], writes=["of1"])
            for k in range(8):
                if final:
                    q = rot("stage", 2)
                    add("dve", (lambda k=k, q=q: lambda e: e.scalar_tensor_tensor(out=stage[q][:], in0=x[:, k, tk], scalar=fnormg_sb[:, k:k + 1],
                                                                                   in1=rstd_t[:], op0=ALU.mult, op1=ALU.mult))(),
                        reads=[("x", sname, b), "of1", "fnormg"], writes=[("stage", q)])
                    dst = (ysT_d if sname == "S" else ypT_d)[k * 128:(k + 1) * 128, tk]
                    store(dst, stage[q][:], ("stage", q))
                else:
                    q = rot("tmp", 2)
                    add("dve", (lambda k=k, q=q: lambda e: e.tensor_tensor(out=tmpf[q][:], in0=x[:, k, tk], in1=rstd_t[:], op=ALU.mult))(),
                        reads=[("x", sname, b), "of1"], writes=[("stage", q)])
                    add("dve", (lambda k=k, q=q: lambda e: e.tensor_scalar(out=hT[sname][:, k, tk], in0=tmpf[q][:],
                                                                            scalar1=gmod[:, k, ci:ci + 1], scalar2=mods[l % 2][:, k, ci:ci + 1],
                                                                            op0=ALU.mult, op1=ALU.add))(),
                        reads=[("stage", q), "gmod"] + [("mod", l % 2, jj) for jj in range(4)], writes=[("h", sname, b)])

        def attend(lhs_of_kt, rhs_q, nq, kts, v_of_kt, scale, OT, okey, col0, kreads, qkey, first=True, last=True):
            n = len(kts)
            slots = [(rot("s", 3), rot("pt", 3)) for _ in range(n)]

            def smm(ii):
                si, pi = slots[ii]
                S = sps[si]
                kt = kts[ii]
                add("pe", lambda e: e.matmul(S[:, 0:nq], lhs_of_kt(kt), rhs_q, start=True, stop=True),
                    reads=list(kreads) + [qkey], writes=[("s", si)], late_keys=(("s", si), qkey))

            for ii in range(min(2, n)):
                smm(ii)
            for ii, kt in enumerate(kts):
                si, pi = slots[ii]
                S = sps[si]
                add("act", lambda e: e.activation(out=PT[pi][:, 0:nq], in_=S[:, 0:nq], func=AF.Exp, scale=scale),
                    reads=[("s", si)], writes=[("pt", pi)])
                if ii + 2 < n:
                    smm(ii + 2)
                add("pe", lambda e: e.matmul(OT[:, col0:col0 + nq], v_of_kt(kt), PT[pi][:, 0:nq],
                                             start=(first and ii == 0), stop=(last and ii == n - 1)),
                    reads=[("pt", pi), "VA"], writes=[okey], late_keys=(("pt", pi), okey))

        def outproj(s, l, sname, b, banks=None):
            ci = 1 if sname == "S" else 0
            tk = slice(b * 512, (b + 1) * 512)
            for cc in range(8):
                bank, bkey = op_banks[rot("opj", 6)] if banks is None else banks[rot("ppj", len(banks))]
                add("pe", lambda e: e.matmul(bank[:], woslot[s][:, cc * 128:(cc + 1) * 128], yTb[:], start=True, stop=True),
                    reads=[("wo", s), "yTb"], writes=[bkey])
                add("dve", lambda e: e.scalar_tensor_tensor(
                    out=xT[sname][:, cc, tk], in0=bank[:], scalar=mods[l % 2][:, 16 + cc, ci:ci + 1], in1=xT[sname][:, cc, tk],
                    op0=ALU.mult, op1=ALU.add),
                    reads=[bkey, ("mod", l % 2, 2), ("mod", l % 2, 3), ("x", sname, b)], writes=[("x", sname, b)])

        def z_block(s, c0, sname, b):
            fm(s, c0, 128, sname, b * 512, 512, pc, "pc")
            silu_from_ps(pc[:], "pc", zs[:], "zs", 512)

        def v_put(src_ps, kt0, nt, eng="act", pkey="pc"):
            v4 = src_ps.rearrange("p (t h d) -> p t h d", t=nt, h=2)
            for h in range(2):
                if eng == "act":
                    add("act", lambda e: e.activation(out=VA[:, kt0:kt0 + nt, h, h * 64:(h + 1) * 64], in_=v4[:, :, h, :], func=AF.Copy),
                        reads=[pkey], writes=["VA"])
                else:
                    add("dve", lambda e: e.tensor_copy(out=VA[:, kt0:kt0 + nt, h, h * 64:(h + 1) * 64], in_=v4[:, :, h, :]),
                        reads=[pkey], writes=["VA"])

        def v_cache(src_d, l, hp):
            for t in range(4):
                for h in range(2):
                    add("pool", lambda e: e.dma_start(
                        out=VA[:, t, h, h * 64:(h + 1) * 64],
                        in_=src_d[l, t * 128:(t + 1) * 128, hp * 128 + h * 64:hp * 128 + (h + 1) * 64]),
                        writes=["VA"], dma="kv")

        def v_block(s, c0, sname, b, l, hp, state_d=None, bank=None):
            pc_, pck = bank if bank is not None else (pc, "pc")
            base = 4 if sname == "S" else 0
            for j in range(4):
                tm(s, c0, 128, sname, b * 512 + j * 128, pc_[:, j * 128:(j + 1) * 128], pck)
            v_put(pc_[:], base + b * 4, 4, pkey=pck)
            if state_d is not None:
                q = rot("stage", 2)
                add("dve", lambda e: e.tensor_copy(out=stage[q][:], in_=pc_[:]), reads=[pck], writes=[("stage", q)])
                for j in range(4):
                    store(state_d[l, j * 128:(j + 1) * 128, hp * 128:(hp + 1) * 128], stage[q][:, j * 128:(j + 1) * 128], ("stage", q))

        def den_rows(hh):
            return slice(64, 128) if hh == 0 else slice(0, 64)

        def normalize_head(OT, okey, hh, dst, dkey, on_act=False):
            hs = slice(hh * 64, (hh + 1) * 64)
            if on_act:
                add("act", lambda e: e.activation(out=rcp[hs, 0:512], in_=OT[den_rows(hh), :], func=AF.Ln), reads=[okey], writes=["rcp"])
                add("act", lambda e: e.activation(out=rcp[hs, 0:512], in_=rcp[hs, 0:512], func=AF.Exp, scale=-1.0),
                    reads=["rcp"], writes=["rcp"])
            else:
                add("dve", lambda e: e.reciprocal(out=rcp[hs, 0:512], in_=OT[den_rows(hh), :]), reads=[okey], writes=["rcp"])
            add("dve", lambda e: e.tensor_tensor(out=dst[hs, :], in0=OT[hs, :], in1=rcp[hs, 0:512], op=ALU.mult),
                reads=[okey, "rcp"], writes=[dkey])

        def attend_set(sname, klhs, qrhs, hh, scale, OT, okey, qkey):
            if sname == "S":
                attend(klhs, qrhs(0, 512), 512, list(range(20)), lambda kt: VA[:, kt, hh, :], scale, OT, okey, 0, ["KT"], qkey)
            else:
                for sq_ in range(2):
                    attend(klhs, qrhs(sq_ * 256, 256), 256, [2 * sq_, 2 * sq_ + 1], lambda kt: VA[:, kt, hh, :], scale, OT, okey,
                           sq_ * 256, ["KT"], qkey, first=(sq_ == 0), last=(sq_ == 1))

        def phase_A(idx):
            kind, l, hp = items[idx]
            s = idx % 2
            sc = 32 ** -0.5
            for sname in ("P", "S"):
                L = LCTX if sname == "S" else 0
                if sname == "S":
                    add("pool", lambda e: e.dma_start(out=KT[:, 0, 0:LCTX], in_=cakT_d[l, hp]), writes=["KT"], dma="kv")
                    v_cache(cav_d, l, hp)
                for b in range(NB[sname]):
                    (A_, Ak), (B_, Bk), (C_, Ck) = banksets[b % 2]
                    fm(s, 256, 128, sname, b * 512, 512, A_, Ak)
                    if sname == "S":
                        fm(s, 384, 128, sname, b * 512, 512, B_, Bk)
                        rope(A_[:], Ak, B_[:], Bk, tabA, slice(0, 128), b, KT[:, 0, L + b * 512:L + (b + 1) * 512], "KT")
                    else:
                        q = rot("stage", 2)
                        add("act", lambda e: e.activation(out=stage[q][:], in_=A_[:], func=AF.Copy), reads=[Ak], writes=[("stage", q)])
                        add("dve", lambda e: e.tensor_copy(out=KT[:, 0, 0:512], in_=A_[:]), reads=[Ak], writes=["KT"])
                        store(sakT_d[l, hp * 128:(hp + 1) * 128, :], stage[q][:], ("stage", q))
                    v_block(s, 512, sname, b, l, hp, state_d=(sav_d if sname == "P" else None), bank=(C_, Ck))
                def prologue(b):
                    fm(s, 0, 128, sname, b * 512, 512, pa, "pa")
                    if sname == "S":
                        fm(s, 128, 128, sname, b * 512, 512, pb, "pb")
                        rope(pa[:], "pa", pb[:], "pb", tabA, slice(0, 128), b, tmpf[0][:], ("stage", 0),
                             scratch=((rcp, "rcp"), (of2, "of2")))
                        src, skey = tmpf[0][:], ("stage", 0)
                    else:
                        src, skey = pa[:], "pa"
                    for qi in range(4):
                        add("dve", lambda e: e.tensor_scalar_mul(out=qb[qi][:], in0=src, scalar1=tabA[:, 194 + qi:195 + qi]),
                            reads=[skey, "tabA"], writes=[qkeys[qi]])

                def post(b):
                    z_block(s, 640, sname, b)
                    add("dve", lambda e: e.scalar_tensor_tensor(out=of1[:], in0=t2b[:], scalar=lamv[:, 5:6], in1=of1[:],
                                                                op0=ALU.mult, op1=ALU.add),
                        reads=["t2b", "neglam", "of1"], writes=["of1"])
                    pq = rot("pt", 3)
                    add("act", lambda e: e.activation(out=PT[pq][:], in_=of1[:], func=AF.Square), reads=["of1"], writes=[("pt", pq)])
                    add("pe", lambda e: e.matmul(pa[:], blockones[:], PT[pq][:], start=True, stop=True),
                        reads=["blockones", ("pt", pq)], writes=["pa"])
                    add("act", lambda e: e.activation(out=t2b[:], in_=pa[:], func=AF.Ln, scale=1.0 / 64, bias=eps_t[:, 0:1]),
                        reads=["pa", "eps"], writes=["t2b"])
                    add("act", lambda e: e.activation(out=t2b[:], in_=t2b[:], func=AF.Exp, scale=-0.5), reads=["t2b"], writes=["t2b"])
                    add("dve", lambda e: e.scalar_tensor_tensor(out=of1[:], in0=of1[:], scalar=subgs[:, 0:1], in1=t2b[:],
                                                                op0=ALU.mult, op1=ALU.mult),
                        reads=["of1", "subgs", "t2b"], writes=["of1"])
                    add("dve", lambda e: e.tensor_tensor(out=yTb[:], in0=of1[:], in1=zs[:], op=ALU.mult), reads=["of1", "zs"], writes=["yTb"])
                    outproj(s, l, sname, b, banks=pp_banks)

                tbuf = [of1, t2b]
                tkey = ["of1", "t2b"]
                pending = None
                prologue(0)
                for b in range(NB[sname]):
                    first = True
                    for hh in range(2):
                        for c in range(2):
                            oi = rot("o", 2)
                            OT = ops_[oi]
                            qi = hh * 2 + c
                            attend_set(sname, lambda kt: KT[:, 0, kt * 128:(kt + 1) * 128],
                                       lambda q0, n, qi=qi: qb[qi][:, q0:q0 + n], hh, sc, OT, ("o", oi), qkeys[qi])
                            if first and pending is not None:
                                post(pending)
                                pending = None
                            first = False
                            normalize_head(OT, ("o", oi), hh, tbuf[c], tkey[c], on_act=(sname == "P" or (hh == 1 and c == 1)))
                    if b + 1 < NB[sname]:
                        prologue(b + 1)
                    pending = b
                post(pending)

        def phase_C(idx):
            kind, l, hp = items[idx]
            s = idx % 2
            sc = 96 ** -0.5
            add("pool", lambda e: e.dma_start(out=wuq[:], in_=w_uq_d[l, hp].rearrange("(i p) c -> p i c", p=128)),
                writes=["wuq"], dma="wsm")
            add("pool", lambda e: e.dma_start(out=wukv[:], in_=w_ukv_d[l, hp]), writes=["wukv"], dma="wsm")
            add("dve", lambda e: e.memset(KT[96:128, :, :], 0.0), writes=["KT"])
            for j in range(2):
                add("dve", lambda e: e.memset(qb[j][96:128, :], 0.0), writes=[("qb", j)])
            for sname in ("P", "S"):
                L = LCTX if sname == "S" else 0
                if sname == "S":
                    add("pool", lambda e: e.dma_start(out=CKV[:, 0:LCTX], in_=cckvT_d[l]), writes=["SH"], dma="kv")
                    for j in range(2):
                        add("pool", lambda e: e.dma_start(out=KT[64:96, j, 0:LCTX], in_=ckpeT_d[l]), writes=["KT"], dma="kv")
                for b in range(NB[sname]):
                    tk = slice(L + b * 512, L + (b + 1) * 512)
                    (A_, Ak), (B_, Bk), (C_, Ck) = banksets[b % 2]
                    fm(s, 256, 128, sname, b * 512, 512, A_, Ak)
                    q = rot("sq", 2)
                    add("act", lambda e: e.activation(out=sqb[q][:], in_=A_[:], func=AF.Square), reads=[Ak], writes=[("sq", q)])
                    add("pe", lambda e: e.matmul(B_[:], ones_bf[:], sqb[q][:], start=True, stop=True),
                        reads=["ones", ("sq", q)], writes=[Bk])
                    add("act", lambda e: e.activation(out=rstd_t[:], in_=B_[:], func=AF.Ln, scale=1.0 / 128, bias=eps_t[:, 0:1]),
                        reads=[Bk, "eps"], writes=["of1"])
                    add("act", lambda e: e.activation(out=rstd_t[:], in_=rstd_t[:], func=AF.Exp, scale=-0.5), reads=["of1"], writes=["of1"])
                    qq = rot("stage", 2)
                    add("dve", lambda e: e.scalar_tensor_tensor(out=stage[qq][:], in0=A_[:], scalar=kvng_sb[:, l:l + 1], in1=rstd_t[:],
                                                                op0=ALU.mult, op1=ALU.mult),
                        reads=[Ak, "kvng", "of1"], writes=[("stage", qq)])
                    add("act", lambda e: e.activation(out=CKV[:, tk], in_=stage[qq][:], func=AF.Copy),
                        reads=[("stage", qq)], writes=["SH"])
                    if sname == "P" and hp == 0:
                        store(sckvT_d[l], stage[qq][:], ("stage", qq))
                    fm(s, 384, 128, sname, b * 512, 512, C_, Ck)
                    if sname == "S":
                        fm(s, 512, 128, sname, b * 512, 512, B_, Bk)
                        rope(C_[64:96, :], Ck, B_[64:96, :], Bk, tabC, slice(64, 96), b, KT[64:96, 0, tk], "KT", nrow=32, tkey="tabC")
                        add("pool", lambda e: e.tensor_copy(out=KT[64:96, 1, tk], in_=KT[64:96, 0, tk]), reads=["KT"], writes=["KT"])
                    else:
                        qq2 = rot("stage", 2)
                        add("dve", lambda e: e.tensor_copy(out=stage[qq2][64:96, :], in_=C_[64:96, :]),
                            reads=[Ck], writes=[("stage", qq2)])
                        for j in range(2):
                            add("act", lambda e: e.activation(out=KT[64:96, j, tk], in_=C_[64:96, :], func=AF.Copy),
                                reads=[Ck], writes=["KT"])
                        if hp == 0:
                            store(skpeT_d[l], stage[qq2][64:96, :], ("stage", qq2))
                nkc = (L + NT[sname]) // 512
                for kc in range(nkc):
                    bank, bkey = proj_banks[rot("pj", 2)]
                    add("pe", lambda e: e.matmul(bank[:], wukv[:, 0:128], CKV[:, kc * 512:(kc + 1) * 512], start=True, stop=True),
                        reads=["wukv", "SH"], writes=[bkey])
                    for j in range(2):
                        add("act", lambda e: e.activation(out=KT[0:64, j, kc * 512:(kc + 1) * 512], in_=bank[j * 64:(j + 1) * 64, :],
                                                          func=AF.Copy),
                            reads=[bkey], writes=["KT"])
                for kc in range(nkc):
                    for t in range(4):
                        add("pe", lambda e: e.matmul(pc[:, t * 128:(t + 1) * 128],
                                                     CKV[:, kc * 512 + t * 128:kc * 512 + (t + 1) * 128],
                                                     wukv[:, 128:256], start=True, stop=True),
                            reads=["wukv", "SH"], writes=["pc"])
                    v_put(pc[:], kc * 4, 4, eng="dve")
                for b in range(NB[sname]):
                    fm(s, 0, 128, sname, b * 512, 512, pa, "pa")
                    fm(s, 128, 128, sname, b * 512, 512, pb, "pb")
                    for i, (bank, bkey) in enumerate(((pa, "pa"), (pb, "pb"))):
                        q = rot("sq", 2)
                        add("act", lambda e: e.activation(out=sqb[q][:], in_=bank[:], func=AF.Square),
                            reads=[bkey], writes=[("sq", q)])
                        add("pe", lambda e: e.matmul(pc[:], ones_bf[:], sqb[q][:], start=(i == 0), stop=(i == 1)),
                            reads=["ones", ("sq", q)], writes=["pc"])
                    add("act", lambda e: e.activation(out=rstd_t[:], in_=pc[:], func=AF.Ln, scale=1.0 / 256, bias=eps_t[:, 0:1]),
                        reads=["pc", "eps"], writes=["of1"])
                    add("act", lambda e: e.activation(out=rstd_t[:], in_=rstd_t[:], func=AF.Exp, scale=-0.5), reads=["of1"], writes=["of1"])
                    for i, (bank, bkey) in enumerate(((pa, "pa"), (pb, "pb"))):
                        add("dve", lambda e: e.scalar_tensor_tensor(out=sqb[i][:], in0=bank[:], scalar=qng_sb[:, l, i:i + 1],
                                                                    in1=rstd_t[:], op0=ALU.mult, op1=ALU.mult),
                            reads=[bkey, "qng", "of1"], writes=[("sq", i)])
                    for j in range(2):
                        for i in range(2):
                            add("pe", lambda e: e.matmul(pa[:], wuq[:, i, j * 256:j * 256 + 128], sqb[i][:],
                                                         start=(i == 0), stop=(i == 1)),
                                reads=["wuq", ("sq", i)], writes=["pa"])
                        if sname == "S":
                            for i in range(2):
                                add("pe", lambda e: e.matmul(pb[:], wuq[:, i, j * 256 + 128:j * 256 + 256], sqb[i][:],
                                                             start=(i == 0), stop=(i == 1)),
                                    reads=["wuq", ("sq", i)], writes=["pb"])
                            rope(pa[0:96, :], "pa", pb[0:96, :], "pb", tabC, slice(0, 96), b, qb[j][0:96, :], ("qb", j), nrow=96, tkey="tabC")
                        else:
                            add("act", lambda e: e.activation(out=qb[j][0:96, :], in_=pa[0:96, :], func=AF.Copy),
                                reads=["pa"], writes=[("qb", j)])
                    z_block(s, 640, sname, b)
                    for j in range(2):
                        oi = rot("o", 2)
                        OT = ops_[oi]
                        attend_set(sname, lambda kt, j=j: KT[:, j, kt * 128:(kt + 1) * 128],
                                   lambda q0, n, j=j: qb[j][:, q0:q0 + n], j, sc, OT, ("o", oi), ("qb", j))
                        normalize_head(OT, ("o", oi), j, of1, "of1", on_act=(sname == "P" or j == 1))
                    add("dve", lambda e: e.tensor_tensor(out=yTb[:], in0=of1[:], in1=zs[:], op=ALU.mult), reads=["of1", "zs"], writes=["yTb"])
                    outproj(s, l, sname, b)

        def phase_D(idx):
            kind, l, hp = items[idx]
            s = idx % 2
            sc = 0.125
            if hp == 0:
                add("pool", lambda e: e.dma_start(out=SH[:, 0:4096], in_=ebias_d[l]), writes=["SH"], dma="eb")
                add("dve", lambda e: e.tensor_scalar_mul(out=SH[:, 0:4096], in0=SH[:, 0:4096], scalar1=8.0),
                    reads=["SH"], writes=["SH"])
            for sname in ("P", "S"):
                L = LCTX if sname == "S" else 0
                if sname == "S":
                    add("pool", lambda e: e.dma_start(out=KT[:, 0, 0:LCTX], in_=cdkT_d[l, hp]), writes=["KT"], dma="kv")
                    v_cache(cdv_d, l, hp)
                for b in range(NB[sname]):
                    (A_, Ak), (B_, Bk), (C_, Ck) = banksets[b % 2]
                    fm(s, 128, 128, sname, b * 512, 512, A_, Ak)
                    add("act", lambda e: e.activation(out=KT[:, 0, L + b * 512:L + (b + 1) * 512], in_=A_[:], func=AF.Copy),
                        reads=[Ak], writes=["KT"])
                    if sname == "P":
                        q = rot("stage", 2)
                        add("dve", lambda e: e.tensor_copy(out=stage[q][:], in_=A_[:]), reads=[Ak], writes=[("stage", q)])
                        store(sdkT_d[l, hp * 128:(hp + 1) * 128, :], stage[q][:], ("stage", q))
                    v_block(s, 256, sname, b, l, hp, state_d=(sdv_d if sname == "P" else None), bank=(C_, Ck))
                for b in range(NB[sname]):
                    fm(s, 0, 128, sname, b * 512, 512, pa, "pa")
                    for hh in range(2):
                        add("dve", lambda e: e.tensor_scalar_mul(out=qb[hh][:], in0=pa[:], scalar1=tabA[:, 198 + hh:199 + hh]),
                            reads=["pa", "tabA"], writes=[("qb", hh)])
                    z_block(s, 384, sname, b)
                    for hh in range(2):
                        oi = rot("o", 2)
                        OT = ops_[oi]
                        okey = ("o", oi)
                        hs = slice(hh * 64, (hh + 1) * 64)
                        hglob = 2 * hp + hh
                        klhs = lambda kt: KT[:, 0, kt * 128:(kt + 1) * 128]
                        qh = qb[hh]
                        qkey = ("qb", hh)
                        if sname == "P":
                            attend_set("P", klhs, lambda q0, n, qh=qh: qh[:, q0:q0 + n], hh, sc, OT, okey, qkey)
                        else:
                            attend(klhs, qh[:, :], 512, [0, 1, 2, 3], lambda kt, hh=hh: VA[:, kt, hh, :], sc, OT, okey, 0,
                                   ["KT"], qkey, first=True, last=False)
                            rows = []
                            for r8 in range(8):
                                brow = b * 8 + r8
                                a = min(max(brow - 4, 0), 24)
                                a0 = a - (a % 2)
                                nt = 4 if a % 2 == 0 else 5
                                rows.append((r8, brow, a, a0, nt, rot("s", 3)))

                            def local_s(r8, brow, a, a0, nt, si):
                                S = sps[si]
                                for t in range(nt):
                                    kc0 = LCTX + (a0 + 2 * t) * 64
                                    j0 = 7 + brow - a0 - 2 * t
                                    add("pe", lambda e: e.matmul(
                                        S[:, t * 64:(t + 1) * 64], KT[:, 0, kc0:kc0 + 128], qh[:, r8 * 64:(r8 + 1) * 64],
                                        start=True, stop=False),
                                        reads=["KT", qkey], writes=[("s", si)])
                                    hm = None
                                    if a % 2 == 1 and t == 0:
                                        hm = 0
                                    elif a % 2 == 1 and t == nt - 1:
                                        hm = 1
                                    add("pe", lambda e: e.matmul(
                                        S[:, t * 64:(t + 1) * 64], ident[:], E8[:, hglob, j0, :], start=False, stop=(hm is None)),
                                        reads=["ident", "SH"], writes=[("s", si)])
                                    if hm is not None:
                                        add("pe", lambda e: e.matmul(
                                            S[:, t * 64:(t + 1) * 64], ident[:], HM[:, hm, :], start=False, stop=True),
                                            reads=["ident", "HM"], writes=[("s", si)])

                            local_s(*rows[0])
                            local_s(*rows[1])
                            for (r8, brow, a, a0, nt, si) in rows:
                                S = sps[si]
                                pli = rot("ptl", 2)
                                ptl = PTL[pli]
                                add("act", lambda e: e.activation(
                                    out=ptl[:, 0:nt, :], in_=S[:, 0:nt * 64].rearrange("p (t c) -> p t c", c=64),
                                    func=AF.Exp, scale=sc),
                                    reads=[("s", si)], writes=[("PTL", pli)])
                                if r8 + 2 < 8:
                                    local_s(*rows[r8 + 2])
                                for t in range(nt):
                                    ktile = 4 + (a0 + 2 * t) // 2
                                    ps_ = slice(0, 128)
                                    add("pe", lambda e: e.matmul(
                                        OT[:, r8 * 64:(r8 + 1) * 64], VA[ps_, ktile, hh, :], ptl[ps_, t, :],
                                        start=False, stop=(r8 == 7 and t == nt - 1)),
                                        reads=[("PTL", pli), "VA"], writes=[okey])
                        normalize_head(OT, okey, hh, of1, "of1", on_act=(sname == "P" or hh == 1))
                    add("dve", lambda e: e.tensor_tensor(out=yTb[:], in0=of1[:], in1=zs[:], op=ALU.mult), reads=["of1", "zs"], writes=["yTb"])
                    outproj(s, l, sname, b)

        def phase_B(idx):
            kind, l, half = items[idx]
            s = idx % 2
            w0 = convw_sb[:, l, half, 0:1]
            w1 = convw_sb[:, l, half, 1:2]
            w2 = convw_sb[:, l, half, 2:3]
            for sname in ("P", "S"):
                def gview(b, off, sname=sname):
                    if sname == "S":
                        return G_S[:, 1 + b * 512 + off:1 + b * 512 + off + 512]
                    return G_P[:, :, 1 + off:1 + off + 256]
                gkey = "KT" if sname == "S" else "rcp"
                if sname == "S":
                    add("dve", lambda e: e.memset(G_S[:, 0:1], 0.0), writes=[gkey])
                    add("dve", lambda e: e.memset(G_S[:, TS + 1:TS + 2], 0.0), writes=[gkey])
                else:
                    add("dve", lambda e: e.memset(G_P[:, :, 0:1], 0.0), writes=[gkey])
                    add("dve", lambda e: e.memset(G_P[:, :, 257:258], 0.0), writes=[gkey])
                bsets = [[(pa, "pa"), (pb, "pb"), (pc, "pc"), (sps[0], ("s", 0))],
                         [(sps[1], ("s", 1)), (sps[2], ("s", 2)), (ops_[0], ("o", 0)), (ops_[1], ("o", 1))]]
                for b in range(NB[sname]):
                    tk = slice(b * 512, (b + 1) * 512)
                    (Pbb, Kbb), (Pbc, Kbc), (Pbh, Kbh), (Pbz, Kbz) = bsets[b % 2]
                    fm(s, 0, 128, sname, b * 512, 512, Pbb, Kbb)
                    fm(s, 128, 128, sname, b * 512, 512, Pbc, Kbc)
                    fm(s, 256, 128, sname, b * 512, 512, Pbh, Kbh)
                    fm(s, 384, 128, sname, b * 512, 512, Pbz, Kbz)
                    add("act", lambda e: e.activation(out=of1[:], in_=Pbc[:], func=AF.Copy), reads=[Kbc], writes=["of1"])
                    gout = gview(b, 0)
                    if sname == "S":
                        add("dve", lambda e: e.tensor_tensor(out=gout, in0=Pbh[:], in1=of1[:], op=ALU.mult),
                            reads=[Kbh, "of1"], writes=[gkey])
                    else:
                        add("dve", lambda e: e.tensor_tensor(out=gout, in0=Pbh[:].rearrange("p (s t) -> p s t", s=2),
                                                             in1=of1[:].rearrange("p (s t) -> p s t", s=2), op=ALU.mult),
                            reads=[Kbh, "of1"], writes=[gkey])
                    silu_from_ps(Pbz[:], Kbz, of1[:], "of1", 512)
                    add("dve", lambda e: e.tensor_tensor(out=BBZ[:, tk], in0=Pbb[:], in1=of1[:], op=ALU.mult),
                        reads=[Kbb, "of1"], writes=["SH"])
                for b in range(NB[sname]):
                    tk = slice(b * 512, (b + 1) * 512)
                    if sname == "S":
                        cv = of1[:]
                    else:
                        cv = of1[:].rearrange("p (s t) -> p s t", s=2)
                    add("dve", (lambda b=b, cv=cv: lambda e: e.tensor_scalar_mul(out=cv, in0=gview(b, 0), scalar1=w1))(),
                        reads=[gkey, "convw"], writes=["of1"])
                    add("dve", (lambda b=b, cv=cv: lambda e: e.scalar_tensor_tensor(out=cv, in0=gview(b, -1), scalar=w0, in1=cv,
                                                                                     op0=ALU.mult, op1=ALU.add))(),
                        reads=[gkey, "convw", "of1"], writes=["of1"])
                    add("dve", (lambda b=b, cv=cv: lambda e: e.scalar_tensor_tensor(out=cv, in0=gview(b, 1), scalar=w2, in1=cv,
                                                                                     op0=ALU.mult, op1=ALU.add))(),
                        reads=[gkey, "convw", "of1"], writes=["of1"])
                    add("dve", (lambda tk=tk: lambda e: e.tensor_tensor(out=yTb[:], in0=of1[:], in1=BBZ[:, tk], op=ALU.mult))(),
                        reads=["of1", "SH"], writes=["yTb"])
                    outproj(s, l, sname, b)

        PH = {"A": phase_A, "B": phase_B, "C": phase_C, "D": phase_D}

        load_item(0)
        for idx, (kind, l, j) in enumerate(items):
            if idx + 1 < len(items):
                load_item(idx + 1)
            if kind == "ada":
                ada_piece(idx)
            else:
                if kind == "A" and j == 0:
                    layer_prep(l)
                    norm_block(l, "P", 0)
                    for b in range(4):
                        norm_block(l, "S", b)
                PH[kind](idx)
        norm_block(0, "P", 0, final=True)
        for b in range(4):
            norm_block(0, "S", b, final=True)
        P.emit(st)
    return nc


_OFF = {"aq": 0, "ak": 256, "av": 512, "az": 768, "bb": 1024, "bc": 1280, "bh": 1536, "bz": 1792,
        "cq": 2048, "ckv": 2304, "kpe": 2432, "cz": 2464, "dq": 2720, "dk": 2976, "dv": 3232, "dz": 3488}


def _perm32():
    perm = np.zeros(32, np.int64)
    sign = np.zeros(32, np.float32)
    for a in range(2):
        for s in range(2):
            for i in range(8):
                perm[a * 16 + s * 8 + i] = a * 16 + (1 - s) * 8 + i
                sign[a * 16 + s * 8 + i] = -1.0 if s == 0 else 1.0
    return perm, sign


def _rope_rows():
    perm, sign = _perm32()
    half = 16
    inv = (1.0 / (np.float32(10000.0) ** (np.arange(0, half, 2, dtype=np.float32) / np.float32(half)))).astype(np.float32)
    rows = np.arange(32, dtype=np.float32)
    cols = np.arange(64, dtype=np.float32)
    CR = np.ones((32, 32), np.float32)
    SR = np.ones((32, 32), np.float32)
    CC = np.ones((32, 64), np.float32)
    SC = np.ones((32, 64), np.float32)
    for a in range(2):
        for s in range(2):
            for i in range(8):
                f = a * 16 + s * 8 + i
                if a == 0:
                    ang = (rows * inv[i]).astype(np.float32)
                    CR[f] = np.cos(ang)
                    SR[f] = sign[f] * np.sin(ang)
                else:
                    ang = (cols * inv[i]).astype(np.float32)
                    CC[f] = np.cos(ang)
                    SC[f] = sign[f] * np.sin(ang)
    return CR, SR, CC, SC


def _prep_shared(inp):
    perm, sign = _perm32()
    w_in = np.asarray(inp["w_in"], np.float32)
    sh = {}
    ada_w = np.asarray(inp["ada_w"], np.float32)
    sh["w_ada"] = np.ascontiguousarray(ada_w.reshape(DEPTH, 1024, 4, 768).transpose(0, 2, 1, 3))
    sh["adabT"] = np.ascontiguousarray(np.asarray(inp["ada_b"], np.float32).reshape(DEPTH, 24, 128).transpose(2, 0, 1).reshape(128, DEPTH * 24))
    sh["normgT"] = np.ascontiguousarray(np.asarray(inp["norm_g"], np.float32).reshape(DEPTH, 8, 128).transpose(2, 0, 1).reshape(128, DEPTH * 8))
    sh["fnormgT"] = np.ascontiguousarray(np.asarray(inp["final_norm_g"], np.float32).reshape(8, 128).T)
    wA = np.zeros((DEPTH, 2, 1024, 768), np.float32)
    wB = np.zeros((DEPTH, 2, 1024, 512), np.float32)
    wC = np.zeros((DEPTH, 2, 1024, 768), np.float32)
    wD = np.zeros((DEPTH, 2, 1024, 512), np.float32)
    p128 = np.concatenate([perm + 32 * i for i in range(4)])
    for hp in range(2):
        base = hp * 128
        qc = _OFF["aq"] + base + np.arange(128)
        kc = _OFF["ak"] + base + np.arange(128)
        wA[:, hp, :, 0:128] = w_in[:, :, qc]
        wA[:, hp, :, 128:256] = w_in[:, :, _OFF["aq"] + base + p128]
        wA[:, hp, :, 256:384] = w_in[:, :, kc]
        wA[:, hp, :, 384:512] = w_in[:, :, _OFF["ak"] + base + p128]
        wA[:, hp, :, 512:640] = w_in[:, :, _OFF["av"] + base:_OFF["av"] + base + 128]
        wA[:, hp, :, 640:768] = w_in[:, :, _OFF["az"] + base:_OFF["az"] + base + 128]
        for i, nm in enumerate(("bb", "bc", "bh", "bz")):
            wB[:, hp, :, i * 128:(i + 1) * 128] = w_in[:, :, _OFF[nm] + base:_OFF[nm] + base + 128]
        wC[:, hp, :, 0:256] = w_in[:, :, _OFF["cq"]:_OFF["cq"] + 256]
        wC[:, hp, :, 256:384] = w_in[:, :, _OFF["ckv"]:_OFF["ckv"] + 128]
        kpe_cols = _OFF["kpe"] + np.arange(32)
        for q4 in range(4):
            wC[:, hp, :, 384 + q4 * 32:384 + (q4 + 1) * 32] = w_in[:, :, kpe_cols]
            wC[:, hp, :, 512 + q4 * 32:512 + (q4 + 1) * 32] = w_in[:, :, kpe_cols]
        wC[:, hp, :, 512 + 64:512 + 96] = w_in[:, :, _OFF["kpe"] + perm]
        wC[:, hp, :, 640:768] = w_in[:, :, _OFF["cz"] + base:_OFF["cz"] + base + 128]
        for i, nm in enumerate(("dq", "dk", "dv", "dz")):
            wD[:, hp, :, i * 128:(i + 1) * 128] = w_in[:, :, _OFF[nm] + base:_OFF[nm] + base + 128]
    sh["w_A"], sh["w_B"], sh["w_C"], sh["w_D"] = wA, wB, wC, wD
    sh["w_out"] = np.ascontiguousarray(np.asarray(inp["w_out"], np.float32))
    wuq_in = np.asarray(inp["mla_w_uq"], np.float32)
    wuq = np.zeros((DEPTH, 2, 256, 512), np.float32)
    for hp in range(2):
        for j in range(2):
            h = 2 * hp + j
            wuq[:, hp, :, j * 256:j * 256 + 96] = wuq_in[:, :, h * 96:(h + 1) * 96]
            wuq[:, hp, :, j * 256 + 128:j * 256 + 192] = wuq_in[:, :, h * 96:h * 96 + 64]
            wuq[:, hp, :, j * 256 + 192:j * 256 + 224] = wuq_in[:, :, h * 96 + 64 + perm]
    sh["w_uq"] = wuq
    wukv_in = np.asarray(inp["mla_w_ukv"], np.float32)
    wukv = np.zeros((DEPTH, 2, 128, 256), np.float32)
    for hp in range(2):
        for j in range(2):
            h = 2 * hp + j
            wukv[:, hp, :, j * 64:(j + 1) * 64] = wukv_in[:, :, h * 128:h * 128 + 64]
            wukv[:, hp, :, 128 + j * 64:128 + (j + 1) * 64] = wukv_in[:, :, h * 128 + 64:h * 128 + 128]
    sh["w_ukv"] = wukv
    sh["qng"] = np.ascontiguousarray(np.asarray(inp["mla_q_norm_g"], np.float32).reshape(DEPTH, 2, 128).transpose(2, 0, 1).reshape(128, DEPTH * 2))
    sh["kvng"] = np.ascontiguousarray(np.asarray(inp["mla_kv_norm_g"], np.float32).T)
    sh["lam"] = np.ascontiguousarray(np.asarray(inp["da_lambda"], np.float32).reshape(DEPTH, 128))
    sh["subgc"] = np.ascontiguousarray(np.tile(np.asarray(inp["da_subln_g"], np.float32), (1, 2)).T)
    cw = np.asarray(inp["conv_w"], np.float32)
    sh["convw"] = np.ascontiguousarray(cw.reshape(DEPTH, 3, 2, 128).transpose(3, 0, 2, 1).reshape(128, DEPTH * 6))
    rpb = np.asarray(inp["na_rpb"], np.float32)
    kc = np.arange(64)[:, None]
    qc = np.arange(64)[None, :]
    ws = np.clip(qc - 8, 0, 48)
    inwin = (kc >= ws) & (kc < ws + 16)
    dx = np.clip(kc - qc + 15, 0, 30)
    eb = np.full((DEPTH, 2, 64, 4, 16, 64), -30000.0, np.float32)
    for half in range(2):
        for j in range(16):
            dy = 14 - j + half
            if 0 <= dy <= 14:
                g = rpb[:, :, dy, :][:, :, dx]
                g = np.where(inwin[None, None], g, np.float32(-30000.0))
                eb[:, half, :, :, j, :] = g.transpose(0, 2, 1, 3)
    sh["ebias"] = np.ascontiguousarray(eb.reshape(DEPTH, 128, 4 * 16 * 64))
    CR, SR, CC, SC = _rope_rows()
    tabA = np.zeros((128, 200), np.float32)
    for hh in range(2):
        for c in range(2):
            r = slice(hh * 64 + c * 32, hh * 64 + c * 32 + 32)
            tabA[r, 0:32] = CR
            tabA[r, 32:64] = SR
            tabA[r, 64:128] = CC
            tabA[r, 128:192] = SC
            tabA[r, 192 + c] = 1.0
            tabA[r, 194 + hh * 2 + c] = 1.0
            tabA[r, 198 + hh] = 1.0
    sh["tabA"] = tabA
    tabC = np.zeros((96, 192), np.float32)
    tabC[0:64, 0:32] = 1.0
    tabC[0:64, 32:64] = 0.0
    tabC[0:64, 64:128] = 1.0
    tabC[0:64, 128:192] = 1.0
    tabC[64:96, 0:32] = CR
    tabC[64:96, 32:64] = SR
    tabC[64:96, 64:128] = CC
    tabC[64:96, 128:192] = SC
    sh["tabC"] = tabC
    return sh


_NC_CACHE = {}


def kernel(**inp):
    sh = _prep_shared(inp)
    x_prompt = np.asarray(inp["x_prompt"], np.float32)
    x_sample = np.asarray(inp["x_sample"], np.float32)
    c = np.asarray(inp["c"], np.float32)
    c_ctx = np.asarray(inp["c_ctx"], np.float32)
    cak = np.asarray(inp["cache_a_k"], np.float32)
    cav = np.asarray(inp["cache_a_v"], np.float32)
    cckv = np.asarray(inp["cache_c_kv"], np.float32)
    ckpe = np.asarray(inp["cache_c_kpe"], np.float32)
    cdk = np.asarray(inp["cache_d_k"], np.float32)
    cdv = np.asarray(inp["cache_d_v"], np.float32)
    ncores_run = NCORES
    in_maps = []
    for core in range(ncores_run):
        b = core // 2
        m = dict(sh)
        m["xpT"] = np.ascontiguousarray(x_prompt[2 * core:2 * core + 2].reshape(TP, 1024).T)
        m["xsT"] = np.ascontiguousarray(x_sample[b].T)
        m["condT"] = np.ascontiguousarray(np.stack([c_ctx, c[b]], axis=1))
        m["cakT"] = np.ascontiguousarray(cak[b].reshape(DEPTH, LCTX, 2, 128).transpose(0, 2, 3, 1))
        m["cav"] = np.ascontiguousarray(cav[b].reshape(DEPTH, LCTX, 256))
        m["cckvT"] = np.ascontiguousarray(cckv[b].transpose(0, 2, 1))
        m["ckpeT"] = np.ascontiguousarray(ckpe[b].transpose(0, 2, 1))
        m["cdkT"] = np.ascontiguousarray(cdk[b].reshape(DEPTH, LCTX, 2, 128).transpose(0, 2, 3, 1))
        m["cdv"] = np.ascontiguousarray(cdv[b].reshape(DEPTH, LCTX, 256))
        in_maps.append(m)
    if "nc" not in _NC_CACHE:
        _NC_CACHE["nc"] = build_program()
    res = run_bass_kernel_spmd(_NC_CACHE["nc"], in_maps, core_ids=list(range(ncores_run)))
    R = res.results
    y_prompt = np.zeros((16, 256, 1024), np.float32)
    y_sample = np.zeros((4, TS, 1024), np.float32)
    s_ak = np.zeros((16, DEPTH, 256, 4, 64), np.float32)
    s_av = np.zeros((16, DEPTH, 256, 4, 64), np.float32)
    s_ckv = np.zeros((16, DEPTH, 256, 128), np.float32)
    s_kpe = np.zeros((16, DEPTH, 256, 32), np.float32)
    s_dk = np.zeros((16, DEPTH, 256, 4, 64), np.float32)
    s_dv = np.zeros((16, DEPTH, 256, 4, 64), np.float32)
    for core in range(ncores_run):
        r = R[core]
        y_prompt[2 * core:2 * core + 2] = np.asarray(r["ypT"]).T.reshape(2, 256, 1024)
        if core % 2 == 0:
            y_sample[core // 2] = np.asarray(r["ysT"]).T
        for (dst, key, tmaj) in ((s_ak, "sakT", False), (s_av, "sav", True), (s_dk, "sdkT", False), (s_dv, "sdv", True)):
            a = np.asarray(r[key])
            if not tmaj:
                a = a.transpose(0, 2, 1)
            a = a.reshape(DEPTH, 2, 256, 4, 64).transpose(1, 0, 2, 3, 4)
            dst[2 * core:2 * core + 2] = a
        a = np.asarray(r["sckvT"]).transpose(0, 2, 1).reshape(DEPTH, 2, 256, 128).transpose(1, 0, 2, 3)
        s_ckv[2 * core:2 * core + 2] = a
        a = np.asarray(r["skpeT"]).transpose(0, 2, 1).reshape(DEPTH, 2, 256, 32).transpose(1, 0, 2, 3)
        s_kpe[2 * core:2 * core + 2] = a
    return (y_prompt, y_sample, s_ak, s_av, s_ckv, s_kpe, s_dk, s_dv)
```
